# Optimizing a Trainium2 kernel written in Bass

```python
import jax
import jax.numpy as jnp
from jax import lax
import numpy as np

D_MODEL = 4096
BATCH = 2
SEQ = 4096
DEPTH = 2

GRID_W = 64
CTX_LEN = 256
DK_A = 128
DV_A = 128
N_HEADS_A = D_MODEL // (2 * DK_A)
WIDTH_A = N_HEADS_A * DK_A
WIDTH_B = D_MODEL // 2
CONV_W = 3
D_FF = 4 * D_MODEL
CHUNK = 64
A_COLS = 5 * WIDTH_A
A_STATE_COLS = 3 * WIDTH_A
B_COLS = 3 * WIDTH_B
G_COLS = 2 * D_MODEL
N_IN = A_COLS + B_COLS + G_COLS
N_MOD = 6 * D_MODEL
ALPHA = (2.0 * DEPTH) ** 0.25
BETA = (8.0 * DEPTH) ** -0.25
LN_EPS = 1e-5
RMS_EPS = 1e-6

kernel_name = "hgrn2_shortconv_parallel_dit_block"


def _heads(t):
    return t.reshape(t.shape[:-1] + (N_HEADS_A, t.shape[-1] // N_HEADS_A))


def _flip(t):
    return jnp.flip(t, axis=1)


def _layer_norm(x, g, b):
    xf = x.astype(jnp.float32)
    mu = jnp.mean(xf, axis=-1, keepdims=True)
    var = jnp.mean(jnp.square(xf - mu), axis=-1, keepdims=True)
    y = (xf - mu) * lax.rsqrt(var + LN_EPS) * g.astype(jnp.float32) + b.astype(jnp.float32)
    return y.astype(x.dtype)


def _hgrn2_forget(f_raw, lb):
    z = _heads(f_raw).astype(jnp.float32)
    lb = lb.reshape(N_HEADS_A, DK_A)
    log_f = jnp.logaddexp(jnp.log(lb), jnp.log1p(-lb) + jax.nn.log_sigmoid(z))
    k = (1.0 - lb) * jax.nn.sigmoid(-z)
    return log_f, k


def _gla_chunkwise(q, k, log_f, v, s0):
    bsz, length, h, _ = q.shape
    dv = v.shape[-1]
    n = length // CHUNK

    def to_chunks(t):
        return t.reshape(bsz, n, CHUNK, h, t.shape[-1]).transpose(1, 0, 3, 2, 4)

    mask = jnp.tril(jnp.ones((CHUNK, CHUNK), dtype=bool))[:, :, None]

    def step(s, inp):
        qi, ki, gi, vi = inp
        b = jnp.cumsum(gi, axis=2)
        diff = b[:, :, :, None, :] - b[:, :, None, :, :]
        decay = jnp.exp(jnp.where(mask, diff, -jnp.inf))
        att = jnp.einsum('bhtsd,bhsd->bhts', decay * qi[:, :, :, None, :], ki)
        o = (jnp.einsum('bhts,bhsv->bhtv', att, vi)
             + jnp.einsum('bhtd,bhdv->bhtv', qi * jnp.exp(b), s))
        b_last = b[:, :, -1:, :]
        s_new = (jnp.exp(b_last[:, :, 0, :])[..., None] * s
                 + jnp.einsum('bhsd,bhsv->bhdv', ki * jnp.exp(b_last - b), vi))
        return s_new, o

    s_fin, o = lax.scan(step, s0, (to_chunks(q), to_chunks(k), to_chunks(log_f), to_chunks(v)))
    o = o.transpose(1, 0, 3, 2, 4).reshape(bsz, length, h, dv)
    return o, s_fin


def _gla_final_state(k, log_f, v):
    b = jnp.cumsum(log_f, axis=1)
    w = jnp.exp(b[:, -1:] - b) * k
    return jnp.einsum('blhd,blhv->bhdv', w, v)


def _hgrn2_mix(pa, s_f0, s_b0, lb_f, lb_b, gn_g):
    f_fwd, f_bwd, i_v, q_raw, g_raw = jnp.split(pa, 5, axis=-1)
    v = _heads(i_v).astype(jnp.float32)
    q = jax.nn.silu(_heads(q_raw).astype(jnp.float32))
    log_ff, k_f = _hgrn2_forget(f_fwd, lb_f)
    log_fb, k_b = _hgrn2_forget(f_bwd, lb_b)
    o_f, s_f = _gla_chunkwise(q, k_f, log_ff, v, s_f0)
    o_b, s_b = _gla_chunkwise(_flip(q), _flip(k_b), _flip(log_fb), _flip(v), s_b0)
    o = o_f + _flip(o_b)
    o = (o * lax.rsqrt(jnp.mean(jnp.square(o), axis=-1, keepdims=True) + RMS_EPS)
         * gn_g.astype(jnp.float32) * jax.nn.silu(_heads(g_raw).astype(jnp.float32)))
    return o.reshape(o.shape[:2] + (WIDTH_A,)).astype(pa.dtype), s_f, s_b


def _hgrn2_ctx_states(pc, lb_f, lb_b):
    f_fwd, f_bwd, i_v = jnp.split(pc, 3, axis=-1)
    v = _heads(i_v).astype(jnp.float32)
    log_ff, k_f = _hgrn2_forget(f_fwd, lb_f)
    log_fb, k_b = _hgrn2_forget(f_bwd, lb_b)
    s_f = _gla_final_state(k_f, log_ff, v)
    s_b = _gla_final_state(_flip(k_b), _flip(log_fb), _flip(v))
    return s_f, s_b


def _conv_centred(z, w, axis):
    n = z.shape[axis]
    half = CONV_W // 2
    pad = [(0, 0)] * z.ndim
    pad[axis] = (half, half)
    zp = jnp.pad(z, pad)
    out = lax.slice_in_dim(zp, 0, n, axis=axis) * w[0]
    for j in range(1, CONV_W):
        out = out + lax.slice_in_dim(zp, j, j + n, axis=axis) * w[j]
    return out


def _short_conv(pb, conv_w, rows):
    b_gate, c_gate, h = jnp.split(pb, 3, axis=-1)
    z = c_gate * h
    if rows is None:
        y = _conv_centred(z, conv_w, axis=1)
    else:
        bsz, seq, width = z.shape
        y = _conv_centred(z.reshape(bsz, rows, GRID_W, width), conv_w, axis=2).reshape(bsz, seq, width)
    return b_gate * y


def _merge(o_a, z_b, pg, w_out_a, w_out_b, w_o):
    g_a, g_b = jnp.split(pg, 2, axis=-1)
    y = jax.nn.sigmoid(g_a) * (o_a @ w_out_a) + jax.nn.sigmoid(g_b) * (z_b @ w_out_b)
    return y @ w_o


def _token_mixer(u, uc, rows, w_in, conv_w, gn_g, w_out_a, w_out_b, w_o, lb_f, lb_b, with_ctx_out):
    bsz = u.shape[0]
    if with_ctx_out:
        zero = jnp.zeros((bsz, N_HEADS_A, DK_A, DV_A), jnp.float32)
        pc = uc @ w_in
        oc, s_f, s_b = _hgrn2_mix(pc[..., :A_COLS], zero, zero, lb_f, lb_b, gn_g)
        zc = _short_conv(pc[..., A_COLS:A_COLS + B_COLS], conv_w, None)
        yc = _merge(oc, zc, pc[..., A_COLS + B_COLS:], w_out_a, w_out_b, w_o)
    else:
        s_f, s_b = _hgrn2_ctx_states(uc @ w_in[:, :A_STATE_COLS], lb_f, lb_b)
        yc = None
    p = u @ w_in
    o, _, _ = _hgrn2_mix(p[..., :A_COLS], s_f, s_b, lb_f, lb_b, gn_g)
    z = _short_conv(p[..., A_COLS:A_COLS + B_COLS], conv_w, rows)
    y = _merge(o, z, p[..., A_COLS + B_COLS:], w_out_a, w_out_b, w_o)
    return y, yc


def _sq_relu_mlp(u, w_up, w_down):
    return jnp.square(jax.nn.relu(u @ w_up)) @ w_down


def setup_inputs(seed: int = 0) -> dict:
    key = jax.random.key(seed)
    ks = jax.random.split(key, 17)
    nrm = jax.random.normal
    f32 = jnp.float32
    return {
        "x": nrm(ks[0], (BATCH, SEQ, D_MODEL), f32),
        "c": nrm(ks[1], (BATCH, D_MODEL), f32),
        "ctx": nrm(ks[2], (BATCH, CTX_LEN, D_MODEL), f32),
        "c_ctx": nrm(ks[3], (D_MODEL,), f32),
        "w_ada": nrm(ks[4], (DEPTH, D_MODEL, N_MOD), f32) * (0.5 * D_MODEL ** -0.5),
        "b_ada": nrm(ks[5], (DEPTH, N_MOD), f32) * 0.02,
        "w_in": nrm(ks[6], (DEPTH, D_MODEL, N_IN), f32) * D_MODEL ** -0.5,
        "conv_w": nrm(ks[7], (DEPTH, CONV_W, WIDTH_B), f32) * CONV_W ** -0.5,
        "gnorm_g": 1.0 + 0.02 * nrm(ks[8], (DEPTH, DV_A), f32),
        "w_out_a": nrm(ks[9], (DEPTH, WIDTH_A, D_MODEL), f32) * (BETA * WIDTH_A ** -0.5),
        "w_out_b": nrm(ks[10], (DEPTH, WIDTH_B, D_MODEL), f32) * (BETA * WIDTH_B ** -0.5),
        "w_o": nrm(ks[11], (DEPTH, D_MODEL, D_MODEL), f32) * (BETA * D_MODEL ** -0.5),
        "w_up": nrm(ks[12], (DEPTH, D_MODEL, D_FF), f32) * D_MODEL ** -0.5,
        "w_down": nrm(ks[13], (DEPTH, D_FF, D_MODEL), f32) * (BETA * D_FF ** -0.5),
        "ln_g": 1.0 + 0.02 * nrm(ks[14], (DEPTH, 2, D_MODEL), f32),
        "ln_b": 0.02 * nrm(ks[15], (DEPTH, 2, D_MODEL), f32),
        "lower_bounds": 0.1 * nrm(ks[16], (DEPTH, 2, WIDTH_A), f32),
    }


def reference(x, c, ctx, c_ctx, w_ada, b_ada, w_in, conv_w, gnorm_g, w_out_a, w_out_b, w_o,
              w_up, w_down, ln_g, ln_b, lower_bounds):
    rows = x.shape[1] // GRID_W
    lb = jax.nn.softmax(lower_bounds.astype(jnp.float32), axis=0)
    lb = jnp.cumsum(lb, axis=0) - lb[0]
    silu_c = jax.nn.silu(c)
    silu_cc = jax.nn.silu(c_ctx)
    xc = ctx
    for l in range(DEPTH):
        last = l == DEPTH - 1
        mod = (silu_c @ w_ada[l] + b_ada[l])[:, None, :]
        mod_c = silu_cc @ w_ada[l] + b_ada[l]
        sh1, sc1, g1, sh2, sc2, g2 = jnp.split(mod, 6, axis=-1)
        csh1, csc1, cg1, csh2, csc2, cg2 = jnp.split(mod_c, 6, axis=-1)
        u = x * (1.0 + sc1) + sh1
        uc = xc * (1.0 + csc1) + csh1
        y, yc = _token_mixer(u, uc, rows, w_in[l], conv_w[l], gnorm_g[l], w_out_a[l], w_out_b[l],
                             w_o[l], lb[l, 0], lb[l, 1], not last)
        x = _layer_norm(ALPHA * x + g1 * y, ln_g[l, 0], ln_b[l, 0])
        x = _layer_norm(ALPHA * x + g2 * _sq_relu_mlp(x * (1.0 + sc2) + sh2, w_up[l], w_down[l]),
                        ln_g[l, 1], ln_b[l, 1])
        if not last:
            xc = _layer_norm(ALPHA * xc + cg1 * yc, ln_g[l, 0], ln_b[l, 0])
            xc = _layer_norm(ALPHA * xc + cg2 * _sq_relu_mlp(xc * (1.0 + csc2) + csh2, w_up[l], w_down[l]),
                             ln_g[l, 1], ln_b[l, 1])
    return x
```

```python
from contextlib import ExitStack

import numpy as np
import concourse.bass as bass
import concourse.mybir as mybir
from concourse.bass_utils import run_bass_kernel_spmd

F32 = mybir.dt.float32
BF16 = mybir.dt.bfloat16
AF = mybir.ActivationFunctionType
ALU = mybir.AluOpType
AX = mybir.AxisListType

LN_EPS = 1e-5
RMS_EPS = 1e-6
NCORES = 8
GRP = 4


class _Stop(Exception):
    pass


STOP = [None]


STOPPED = [False]


def stop_at(n):
    if STOP[0] == n:
        STOPPED[0] = True


class Cfg:
    def __init__(self, D=4096, SEQ=4096, CTX=256, DEPTH=2, T=512):
        self.D, self.SEQ, self.CTX, self.DEPTH, self.T = D, SEQ, CTX, DEPTH, T
        self.BATCH = 2
        self.GRID_W = 64
        self.NH = D // 256
        self.WA = self.NH * 128
        self.WB = D // 2
        self.DFF = 4 * D
        self.KC = D // 128
        self.NL = SEQ // GRP
        self.NT = CTX + self.NL
        self.NIN = 5 * self.WA + 3 * self.WB + 2 * D
        self.NMOD = 6 * D
        self.ALPHA = (2.0 * DEPTH) ** 0.25
        self.NCH = self.NT // 128
        assert self.NL % 128 == 0 and CTX % 128 == 0 and T % 128 == 0 and T <= 512
        assert (self.NIN // 128) % GRP == 0 and (6 * self.KC) % GRP == 0 and self.KC % GRP == 0


class Buf:
    __slots__ = ("name", "w", "r")

    def __init__(self, name):
        self.name, self.w, self.r = name, None, {}

    def set_write(self, h):
        self.w, self.r = h, {}

    def add_read(self, h):
        k = id(h[0])
        if k not in self.r or self.r[k][1] < h[1]:
            self.r[k] = h


class Ctx:
    ENG = ("pe", "act", "dve", "pool", "sp")

    def __init__(self, nc, stack):
        self.nc, self.stack = nc, stack
        self.eng = {"pe": nc.tensor, "act": nc.scalar, "dve": nc.vector, "pool": nc.gpsimd, "sp": nc.sync}
        self.nsem = 0
        self.sem = {k: self._newsem() for k in self.ENG}
        self.cnt = {k: 0 for k in self.ENG}
        self.seen = {k: {} for k in self.ENG}
        self.pending = {k: [] for k in self.ENG}
        self.dpool = {k: [[self._newsem(), 0] for _ in range(8)] for k in ("sp", "pool")}
        self.dnext = {"sp": 0, "pool": 0}
        self.ccsem = [self._newsem(), 0]

    def _newsem(self):
        self.nsem += 1
        return self.stack.enter_context(self.nc.semaphore(f"sm{self.nsem}"))

    def _wait(self, k, deps):
        e = self.eng[k]
        for d in deps:
            if d is None:
                continue
            sem, n = d
            key = id(sem)
            if self.seen[k].get(key, 0) >= n:
                continue
            e.wait_ge(sem, n)
            self.seen[k][key] = n

    @staticmethod
    def _deps(r, w, deps):
        out = list(deps)
        for b in r:
            if b.w is not None:
                out.append(b.w)
        for b in w:
            if b.w is not None:
                out.append(b.w)
            out.extend(b.r.values())
        return out

    def op(self, k, fn, r=(), w=(), deps=(), sig=True):
        if STOPPED[0]:
            return None
        self._wait(k, self._deps(r, w, deps))
        ins = fn(self.eng[k])
        if not sig:
            self.pending[k].append((tuple(r), tuple(w)))
            return None
        if self.cnt[k] >= 30000:
            self.sem[k] = self._newsem()
            self.cnt[k] = 0
        ins.then_inc(self.sem[k], 1)
        self.cnt[k] += 1
        h = (self.sem[k], self.cnt[k])
        for (pr, pw) in self.pending[k]:
            for b in pr:
                b.add_read(h)
            for b in pw:
                b.set_write(h)
        self.pending[k] = []
        for b in r:
            b.add_read(h)
        for b in w:
            b.set_write(h)
        return h

    def dma(self, k, out, in_, r=(), w=(), deps=()):
        if STOPPED[0]:
            return None
        slot = self.dpool[k][self.dnext[k]]
        self.dnext[k] = (self.dnext[k] + 1) % len(self.dpool[k])
        if slot[1] >= 30000:
            slot[0], slot[1] = self._newsem(), 0
        prev = (slot[0], slot[1]) if slot[1] > 0 else None
        self._wait(k, self._deps(r, w, list(deps) + [prev]))
        self.eng[k].dma_start(out=out, in_=in_).then_inc(slot[0], 16)
        slot[1] += 16
        h = (slot[0], slot[1])
        for b in r:
            b.add_read(h)
        for b in w:
            b.set_write(h)
        return h

    def allgather(self, groups, src, dst, r=(), w=()):
        if STOPPED[0]:
            return None
        self._wait("pool", self._deps(r, w, ()))
        self.nc.gpsimd.collective_compute("AllGather", ALU.bypass, replica_groups=groups,
                                          ins=[src.opt()], outs=[dst.opt()]).then_inc(self.ccsem[0])
        self.ccsem[1] += 1
        h = (self.ccsem[0], self.ccsem[1])
        for b in r:
            b.add_read(h)
        for b in w:
            b.set_write(h)
        return h

    def barrier(self, bufs=()):
        if STOPPED[0]:
            return
        hs = [(self.sem[k], self.cnt[k]) for k in self.ENG if self.cnt[k] > 0]
        for k in ("sp", "pool"):
            hs += [(s[0], s[1]) for s in self.dpool[k] if s[1] > 0]
        if self.ccsem[1] > 0:
            hs.append((self.ccsem[0], self.ccsem[1]))
        for k in ("pe", "act", "dve", "sp", "pool"):
            self._wait(k, hs)


def apv(ap, off, dims):
    return bass.AP(tensor=ap.tensor, offset=ap.offset + off, ap=[list(ap.ap[0])] + [list(d) for d in dims])


WNAMES = ("w_in", "w_out_a", "w_out_b", "w_o", "w_up", "w_down")


def wshape(cfg, name):
    return {"w_in": (cfg.D, cfg.NIN), "w_out_a": (cfg.WA, cfg.D), "w_out_b": (cfg.WB, cfg.D),
            "w_o": (cfg.D, cfg.D), "w_up": (cfg.D, cfg.DFF), "w_down": (cfg.DFF, cfg.D)}[name]


def build_program(cfg):
    STOPPED[0] = False
    nc = bass.Bass("TRN2", target_bir_lowering=False)
    D, KC, NT, NL, CTX, NH, T, NCH, DEPTH = cfg.D, cfg.KC, cfg.NT, cfg.NL, cfg.CTX, cfg.NH, cfg.T, cfg.NCH, cfg.DEPTH
    DFFC = cfg.DFF // 128
    OCA = 6 * KC // GRP
    ALPHA = float(cfg.ALPHA)
    NCC = CTX // 128
    NLC = NL // 128
    SEGS = ((0, CTX), (CTX, NT))

    def din(name, shape, dt=F32):
        return nc.dram_tensor(name, list(shape), dt, kind="ExternalInput").ap()

    xT = din("xT", [128, KC * NT])
    cvec = din("cvec", [128, KC * 2])
    sel = din("sel", [128, 8])
    wsh = {}
    for l in range(DEPTH):
        for nm in WNAMES:
            K, N = wshape(cfg, nm)
            wsh[(l, nm)] = din(f"{nm}_{l}", [N, K])
    wada = [din(f"w_ada_{l}", [OCA * 128, D]) for l in range(DEPTH)]
    bada = din("b_ada", [128, DEPTH * OCA])
    convw = din("conv_w", [128, DEPTH * NH * 3])
    gng = din("gnorm_g", [128, DEPTH])
    lng = din("ln_g", [128, DEPTH * 2 * KC])
    lnb = din("ln_b", [128, DEPTH * 2 * KC])
    lbr = din("lbraw", [128, DEPTH * 2 * NH])
    outT = nc.dram_tensor("outT", [128, KC * NL], F32, kind="ExternalOutput").ap()

    wfull = wsh
    mod_src = nc.dram_tensor("mod_src", [128, DEPTH * OCA * 2], F32).ap()
    mod_all = nc.dram_tensor("mod_all", [GRP * 128, DEPTH * OCA * 2], F32).ap()
    xs_d = nc.dram_tensor("xs_d", [128, KC * NT], F32).ap()
    oa_d = nc.dram_tensor("oa_d", [128, NH * NT], BF16).ap()
    zb_d = nc.dram_tensor("zb_d", [128, NH * NT], BF16).ap()
    XW = 2 * 128 + 2
    xch_src = [nc.dram_tensor(f"xch_src{i}", [128, XW], F32).ap() for i in range(DEPTH * NH)]
    xch_dst = [nc.dram_tensor(f"xch_dst{i}", [GRP * 128, XW], F32).ap() for i in range(DEPTH * NH)]

    B_xs, B_oa, B_zb = Buf("xs_d"), Buf("oa_d"), Buf("zb_d")
    B_wfull = {k: Buf(f"wfull{k}") for k in wfull}

    with ExitStack() as st:
        C = Ctx(nc, st)

        uniq = [0]

        def sb(stack, name, shape, dt):
            uniq[0] += 1
            return stack.enter_context(nc.sbuf_tensor(f"{name}_{uniq[0]}", list(shape), dt))

        def pst(name, shape, dt):
            return st.enter_context(nc.psum_tensor(name, list(shape), dt))

        ACC = [pst(f"acc{i}", [128, 512], F32) for i in range(4)]
        B_ACC = [[Buf(f"acc{i}")] for i in range(4)]
        PSA = pst("psA", [128, 512], F32)
        PSB = pst("psB", [128, 512], F32)
        PSTR = pst("psT", [128, 1024], BF16)
        PSS = pst("psS", [128, 512], F32)
        B_PSA, B_PSB, B_PSS = Buf("psA"), Buf("psB"), Buf("psS")
        _bpt = Buf("psT")
        B_PSTR = [_bpt, _bpt]

        ident_f = sb(st, "ident_f", [128, 128], F32)
        ident = sb(st, "ident", [128, 128], BF16)
        ones_f = sb(st, "ones_f", [128, 128], F32)
        cmask = sb(st, "cmask", [128, 128], F32)
        mk128 = sb(st, "mk128", [128, 128], F32)
        mk32 = sb(st, "mk32", [128, 32], F32)
        MOD = sb(st, "MOD", [128, DEPTH * 2 * 6 * KC], F32)
        LB = sb(st, "LB", [128, DEPTH * 2 * NH], F32)
        OML = sb(st, "OML", [128, DEPTH * 2 * NH], F32)
        CW = sb(st, "CW", [128, DEPTH * NH * 3], F32)
        GN = sb(st, "GN", [128, DEPTH], F32)
        LNG = sb(st, "LNG", [128, DEPTH * 2 * KC], F32)
        LNB = sb(st, "LNB", [128, DEPTH * 2 * KC], F32)
        SEL = sb(st, "SEL", [128, 8], F32)
        zeros_b = sb(st, "zeros_b", [128, 128], BF16)
        B_const = Buf("const")
        B_MOD = Buf("MOD")

        def modcol(l, j, part, c):
            i = ((l * 2 + j) * 6 + part) * KC + c
            return MOD[:, i:i + 1]

        SLABW = KC * 128
        slabs, B_slab = [], []
        slab_i = [0]

        def alloc_slabs(stack, n):
            slabs[:] = [sb(stack, f"slab{i}", [128, SLABW], BF16) for i in range(n)]
            B_slab[:] = [Buf(f"slab{i}") for i in range(n)]
            slab_i[0] = 0

        def load_slab(src2d, bsrc, n):
            i = slab_i[0]
            slab_i[0] = (i + 1) % len(slabs)
            C.dma("pool", slabs[i][:, 0:n], src2d, r=[bsrc], w=[B_slab[i]])
            return slabs[i], B_slab[i]

        def run_jobs(jobs, depth=1):
            n = len(jobs)
            loaded = [jobs[i][0]() for i in range(min(depth, n))]
            for i in range(n):
                jobs[i][1](loaded[i])
                loaded[i] = None
                if i + depth < n:
                    loaded.append(jobs[i + depth][0]())

        def mm_acc(ps_ap, bps, pairs, rbufs):
            n = len(pairs)
            h = None
            for i, (l_, r_) in enumerate(pairs):
                h = C.op("pe", lambda e: e.matmul(ps_ap, l_, r_, start=(i == 0), stop=(i == n - 1)),
                         r=rbufs, w=[bps], sig=(i == n - 1))
            return h

        try:
            C.op("pool", lambda e: e.memset(ident_f[:], 0.0), w=[B_const])
            C.op("pool", lambda e: e.affine_select(ident_f[:], ident_f[:], pattern=[[-1, 128]], compare_op=ALU.not_equal,
                                                   fill=1.0, base=0, channel_multiplier=1), w=[B_const])
            C.op("pool", lambda e: e.memset(ones_f[:], 1.0), w=[B_const])
            C.op("pool", lambda e: e.memset(cmask[:], 1.0), w=[B_const])
            C.op("pool", lambda e: e.affine_select(cmask[:], cmask[:], pattern=[[1, 128]], compare_op=ALU.is_ge,
                                                   fill=0.0, base=0, channel_multiplier=-1), w=[B_const])
            C.op("pool", lambda e: e.memset(mk128[:], 1.0), w=[B_const])
            C.op("pool", lambda e: e.memset(mk32[:], 1.0), w=[B_const])
            C.op("pool", lambda e: e.memset(mk128[:, 0:1], 0.0), w=[B_const])
            C.op("pool", lambda e: e.memset(mk32[:, 0:1], 0.0), w=[B_const])
            C.op("pool", lambda e: e.memset(zeros_b[:], 0.0), w=[B_const])
            C.op("dve", lambda e: e.tensor_copy(ident[:], ident_f[:]), r=[B_const], w=[B_const])
            for (dst_, src_) in ((CW, convw), (GN, gng), (LNG, lng), (LNB, lnb), (SEL, sel), (LB, lbr)):
                C.dma("sp", dst_[:], src_, w=[B_const])

            allg = [[0, 1, 2, 3], [4, 5, 6, 7]]
            wqueue = []

            def emit_weight():
                return

            def ensure_weights(l):
                while wqueue and wqueue[0][0] <= l:
                    emit_weight()

            with ExitStack() as ph:
                W = 2 * NH
                ex = sb(ph, "lb_ex", [128, DEPTH * W], F32)
                den = sb(ph, "lb_den", [128, W], F32)
                Bt = Buf("lbtmp")
                C.op("act", lambda e: e.activation(ex[:], LB[:], AF.Exp), r=[B_const], w=[Bt])
                C.op("dve", lambda e: e.tensor_copy(den[:], ex[:, 0:W]), r=[Bt], w=[Bt])
                for l in range(1, DEPTH):
                    C.op("dve", lambda e: e.tensor_tensor(den[:], den[:], ex[:, l * W:(l + 1) * W], ALU.add), r=[Bt], w=[Bt])
                C.op("dve", lambda e: e.reciprocal(den[:], den[:]), r=[Bt], w=[Bt])
                C.op("dve", lambda e: e.memset(LB[:, 0:W], 0.0), r=[Bt], w=[B_const])
                for l in range(1, DEPTH):
                    C.op("dve", lambda e: e.tensor_tensor(ex[:, l * W:(l + 1) * W], ex[:, l * W:(l + 1) * W], den[:], ALU.mult),
                         r=[Bt], w=[Bt])
                    C.op("dve", lambda e: e.tensor_tensor(LB[:, l * W:(l + 1) * W], LB[:, (l - 1) * W:l * W],
                                                          ex[:, l * W:(l + 1) * W], ALU.add), r=[Bt], w=[B_const])
                C.op("dve", lambda e: e.tensor_scalar(OML[:], LB[:], -1.0, 1.0, ALU.mult, ALU.add), r=[B_const], w=[B_const])
                stop_at(1)

                sc = sb(ph, "silu_c", [128, KC * 2], F32)
                modl = sb(ph, "modl", [128, DEPTH * OCA * 2], F32)
                bad = sb(ph, "bad", [128, DEPTH * OCA], F32)
                wa = [sb(ph, f"wa{i}", [128, D], F32) for i in range(2)]
                B_wa = [Buf("wa0"), Buf("wa1")]
                modg = sb(ph, "modg", [128, GRP * DEPTH * OCA * 2], F32)
                B_sc, B_modl, B_modg = Buf("sc"), Buf("modl"), Buf("modg")
                B_ms, B_ma = Buf("mod_src"), Buf("mod_all")
                C.dma("sp", sc[:], cvec, w=[B_sc])
                C.dma("sp", bad[:], bada, w=[B_sc])
                C.op("act", lambda e: e.activation(sc[:], sc[:], AF.Silu), r=[B_sc], w=[B_sc])
                it = 0
                for l in range(DEPTH):
                    for oc in range(OCA):
                        i = it % 2
                        it += 1
                        C.dma("sp", wa[i][:], wada[l][oc * 128:(oc + 1) * 128, :], w=[B_wa[i]])
                        half = B_ACC[it % 2][0]
                        ps = ACC[it % 2][:, 0:2]
                        mm_acc(ps, half, [(wa[i][:, kc * 128:(kc + 1) * 128], sc[:, kc * 2:(kc + 1) * 2]) for kc in range(KC)],
                               [B_wa[i], B_sc])
                        o0 = (l * OCA + oc) * 2
                        C.op("dve", lambda e: e.tensor_scalar(modl[:, o0:o0 + 2], ps, bad[:, l * OCA + oc:l * OCA + oc + 1], None, ALU.add),
                             r=[half, B_sc], w=[B_modl])
                stop_at(2)
                C.dma("sp", mod_src, modl[:], r=[B_modl], w=[B_ms])
                C.allgather(allg, mod_src, mod_all, r=[B_ms], w=[B_ma])
                C.dma("sp", apv(modg[:], 0, [[DEPTH * OCA * 2, GRP], [1, DEPTH * OCA * 2]]),
                      mod_all.rearrange("(r p) n -> p r n", p=128), r=[B_ma], w=[B_modg])
                for l in range(DEPTH):
                    for jj in range(2):
                        dst = apv(MOD[:], (l * 2 + jj) * 6 * KC, [[OCA, GRP], [1, OCA]])
                        src = apv(modg[:], l * OCA * 2 + jj, [[DEPTH * OCA * 2, GRP], [2, OCA]])
                        C.op("dve", lambda e: e.tensor_copy(dst, src), r=[B_modg], w=[B_MOD])
                        for part in (1, 4):
                            a = modcol(l, jj, part, 0)
                            a = apv(a, 0, [[1, KC]])
                            C.op("dve", lambda e: e.tensor_scalar(a, a, 1.0, None, ALU.add), w=[B_MOD])
                C.barrier()
                stop_at(3)

            for l in range(DEPTH):
                last = l == DEPTH - 1
                x_src = xT if l == 0 else xs_d
                Win, Wa_, Wb_, Wo_, Wup, Wdn = (wfull[(l, nm)] for nm in WNAMES)
                Bw = {nm: B_wfull[(l, nm)] for nm in WNAMES}

                with ExitStack() as ph:
                    ensure_weights(l - 1)
                    stop_at(30)
                    if wqueue and wqueue[0] == (l, "w_in"):
                        emit_weight()
                    stop_at(31)
                    alloc_slabs(ph, 2)
                    u_sb = sb(ph, "u_sb", [128, KC * NT], BF16)
                    B_u = Buf("u_sb")

                    def fbuf(name, dt=F32, w=NT):
                        return sb(ph, name, [128, w], dt), Buf(name)

                    ZR0 = fbuf("zr0")
                    ZS = fbuf("zs")
                    W1 = fbuf("w1")
                    W2 = fbuf("w2")
                    B128 = fbuf("b128")
                    OIN = [fbuf("oin0"), fbuf("oin1")]
                    US = [fbuf("us0", BF16), fbuf("us1", BF16)]
                    QF = fbuf("qf", BF16)
                    GS = fbuf("gs", BF16)
                    QS = fbuf("qs", BF16)
                    KF = fbuf("kf", BF16)
                    VB = fbuf("vb", BF16)
                    VS = fbuf("vs", BF16)
                    QSUB = fbuf("qsub", BF16)
                    OAB = QSUB
                    KD = fbuf("kd", BF16)
                    ATT = fbuf("att", BF16)
                    VT = fbuf("vt", BF16)
                    KDT = fbuf("kdt", BF16)
                    QEK = [fbuf("qek0", BF16), fbuf("qek1", BF16)]
                    SSTD = [fbuf("sst0", BF16), fbuf("sst1", BF16)]
                    KEYS = fbuf("keys", BF16, NCH * 320)
                    DEC = fbuf("dec", F32, NCH)
                    DECS = [fbuf("decs0", F32, NCH), fbuf("decs1", F32, NCH)]
                    BLS = fbuf("bls", F32, 1)
                    S_ = fbuf("S", F32, 128)
                    S2 = fbuf("S2", F32, 128)
                    SCTX = [fbuf("sctx0", F32, 128), fbuf("sctx1", F32, 128)]
                    XCH = fbuf("xch", F32, XW)
                    XG = fbuf("xg", F32, GRP * XW)
                    SINIT = [fbuf("sinit0", F32, 128), fbuf("sinit1", F32, 128)]
                    P3 = [ZR0, ZS, B128]
                    C.op("dve", lambda e: e.memset(ATT[0][:], 0.0), w=[ATT[1]])
                    M128 = fbuf("m128", BF16)
                    M32 = fbuf("m32", BF16)
                    C.op("pool", lambda e: e.memset(M128[0][:], 1.0), w=[M128[1]])
                    C.op("pool", lambda e: e.memset(M32[0][:], 1.0), w=[M32[1]])
                    C.op("pool", lambda e: e.memset(apv(M128[0][:], 0, [[128, NT // 128], [1, 1]]), 0.0), w=[M128[1]])
                    C.op("pool", lambda e: e.memset(apv(M32[0][:], 0, [[32, NT // 32], [1, 1]]), 0.0), w=[M32[1]])

                    stop_at(32)
                    xl = [W1, W2]
                    for kc in range(KC):
                        i = kc % 2
                        C.dma("sp", xl[i][0][:], x_src[:, kc * NT:(kc + 1) * NT], r=[B_xs], w=[xl[i][1]])
                        for jj, (a, b) in ((1, SEGS[0]), (0, SEGS[1])):
                            C.op("dve", lambda e: e.tensor_scalar(u_sb[:, kc * NT + a:kc * NT + b], xl[i][0][:, a:b],
                                                                  modcol(l, jj, 1, kc), modcol(l, jj, 0, kc), ALU.mult, ALU.add),
                                 r=[xl[i][1], B_MOD], w=[B_u])

                    stop_at(4)
                    TT = [(t0, min(512, NT - t0)) for t0 in range(0, NT, 512)]
                    acc_i = [0]

                    def proj(oc, epi):
                        def ld():
                            return load_slab(Win[oc * 128:(oc + 1) * 128, :], Bw["w_in"], KC * 128)

                        def cp(res):
                            slab, bsl = res
                            for (t0, tn) in TT:
                                a = acc_i[0] % 4
                                acc_i[0] += 1
                                ps = ACC[a][:, 0:tn]
                                bufs = B_ACC[a]
                                for kc in range(KC):
                                    C.op("pe", lambda e: e.matmul(ps, slab[:, kc * 128:(kc + 1) * 128],
                                                                  u_sb[:, kc * NT + t0:kc * NT + t0 + tn],
                                                                  start=(kc == 0), stop=(kc == KC - 1)),
                                         r=[bsl, B_u], w=bufs, sig=(kc == KC - 1))
                                epi(ps, bufs, t0, tn)
                        return (ld, cp)

                    def epi_copy(dst, eng="act", func=None):
                        def f(ps, bufs, t0, tn):
                            if func is not None:
                                C.op("act", lambda e: e.activation(dst[0][:, t0:t0 + tn], ps, func), r=bufs, w=[dst[1]])
                            elif eng == "act":
                                C.op("act", lambda e: e.copy(dst[0][:, t0:t0 + tn], ps), r=bufs, w=[dst[1]])
                            else:
                                C.op("dve", lambda e: e.tensor_copy(dst[0][:, t0:t0 + tn], ps), r=bufs, w=[dst[1]])
                        return f

                    def epi_rev(dst):
                        def f(ps, bufs, t0, tn):
                            for (a, b) in SEGS:
                                p0, p1 = max(t0, a), min(t0 + tn, b)
                                if p0 >= p1:
                                    continue
                                rv = apv(ps, p1 - t0 - 1, [[-1, p1 - p0]])
                                C.op("dve", lambda e: e.tensor_copy(dst[0][:, a + b - p1:a + b - p0], rv), r=bufs, w=[dst[1]])
                        return f

                    def rev_copy(eng, dst, src):
                        for (a, b) in SEGS:
                            rv = apv(src[0][:], b - 1, [[-1, b - a]])
                            if eng == "dve":
                                C.op("dve", lambda e: e.tensor_copy(dst[0][:, a:b], rv), r=[src[1]], w=[dst[1]])
                            else:
                                C.op("act", lambda e: e.copy(dst[0][:, a:b], rv), r=[src[1]], w=[dst[1]])

                    def c3(t, w, off=0, cw=128):
                        return apv(t[:], off, [[cw, NCH], [1, w]])

                    def gla_prep(dr, hd):
                        lbc = LB[:, (l * 2 + dr) * NH + hd:(l * 2 + dr) * NH + hd + 1]
                        omc = OML[:, (l * 2 + dr) * NH + hd:(l * 2 + dr) * NH + hd + 1]
                        if dr == 0:
                            zs, qs, vs = ZR0, QF, VB
                        else:
                            rev_copy("act", QS, QF)
                            rev_copy("dve", VS, VB)
                            zs, qs, vs = ZS, QS, VS
                        C.op("act", lambda e: e.activation(W1[0][:], zs[0][:], AF.Exp, scale=-1.0), r=[zs[1]], w=[W1[1]])
                        C.op("dve", lambda e: e.tensor_scalar(W2[0][:], W1[0][:], 1.0, None, ALU.add), r=[W1[1]], w=[W2[1]])
                        C.op("dve", lambda e: e.reciprocal(W2[0][:], W2[0][:]), w=[W2[1]])
                        C.op("dve", lambda e: e.tensor_scalar(W1[0][:], W2[0][:], omc, lbc, ALU.mult, ALU.add),
                             r=[W2[1], B_const], w=[W1[1]])
                        C.op("act", lambda e: e.activation(W2[0][:], W1[0][:], AF.Ln), r=[W1[1]], w=[W2[1]])
                        C.op("dve", lambda e: e.tensor_scalar(KF[0][:], W1[0][:], -1.0, 1.0, ALU.mult, ALU.add), r=[W1[1]], w=[KF[1]])
                        C.op("dve", lambda e: e.tensor_tensor_scan(B128[0][:], M128[0][:], W2[0][:], 0.0, ALU.mult, ALU.add),
                             r=[W2[1], M128[1]], w=[B128[1]])
                        C.op("dve", lambda e: e.tensor_tensor_scan(W1[0][:], M32[0][:], W2[0][:], 0.0, ALU.mult, ALU.add),
                             r=[W2[1], M32[1]], w=[W1[1]])
                        C.op("act", lambda e: e.activation(W1[0][:], W1[0][:], AF.Exp), r=[W1[1]], w=[W1[1]])
                        C.op("dve", lambda e: e.tensor_tensor(QSUB[0][:], qs[0][:], W1[0][:], ALU.mult), r=[qs[1], W1[1]], w=[QSUB[1]])
                        C.op("act", lambda e: e.activation(W2[0][:], B128[0][:], AF.Exp), r=[B128[1]], w=[W2[1]])
                        C.op("dve", lambda e: e.tensor_tensor(QEK[dr][0][:], qs[0][:], W2[0][:], ALU.mult), r=[qs[1], W2[1]], w=[QEK[dr][1]])
                        C.op("dve", lambda e: e.tensor_copy(DECS[dr][0][:], apv(W2[0][:], 127, [[128, NCH]])),
                             r=[W2[1]], w=[DECS[dr][1]])
                        C.op("dve", lambda e: e.tensor_reduce(BLS[0][:], apv(B128[0][:], CTX + 127, [[128, NLC]]), AX.X, ALU.add),
                             r=[B128[1]], w=[BLS[1]])
                        C.op("act", lambda e: e.activation(XCH[0][:, 256 + dr:257 + dr], BLS[0][:], AF.Exp), r=[BLS[1]], w=[XCH[1]])
                        bl_b = apv(B128[0][:], 127, [[128, NCH], [0, 128]])
                        C.op("dve", lambda e: e.tensor_tensor(c3(W1[0], 128), bl_b, c3(B128[0], 128), ALU.subtract),
                             r=[B128[1]], w=[W1[1]])
                        C.op("act", lambda e: e.activation(W1[0][:], W1[0][:], AF.Exp), r=[W1[1]], w=[W1[1]])
                        C.op("dve", lambda e: e.tensor_tensor(KD[0][:], KF[0][:], W1[0][:], ALU.mult), r=[KF[1], W1[1]], w=[KD[1]])
                        for I in range(4):
                            wI = 32 * (I + 1)
                            offI = (0, 32, 96, 192)[I]
                            tmp = W1 if I % 2 == 0 else W2
                            dstE = apv(tmp[0][:], 0, [[wI, NCH], [1, wI]])
                            if I == 0:
                                C.op("dve", lambda e: e.tensor_scalar(dstE, c3(B128[0], wI), -1.0, None, ALU.mult),
                                     r=[B128[1]], w=[tmp[1]])
                            else:
                                rI = apv(B128[0][:], 32 * I - 1, [[128, NCH], [0, wI]])
                                C.op("dve", lambda e: e.tensor_tensor(dstE, rI, c3(B128[0], wI), ALU.subtract),
                                     r=[B128[1]], w=[tmp[1]])
                            C.op("act", lambda e: e.activation(tmp[0][:, 0:NCH * wI], tmp[0][:, 0:NCH * wI], AF.Exp), w=[tmp[1]])
                            C.op("dve", lambda e: e.tensor_tensor(c3(KEYS[0], wI, offI, 320), dstE, c3(KF[0], wI), ALU.mult),
                                 r=[tmp[1], KF[1]], w=[KEYS[1]])
                        for c0 in range(0, NCH, 4):
                            ncg = min(4, NCH - c0)
                            for which, (src_, dst_) in enumerate(((vs, VT), (KD, KDT))):
                                slot = which
                                for cc in range(ncg):
                                    c = c0 + cc
                                    C.op("pe", lambda e: e.transpose(PSTR[:, slot * 512 + cc * 128:slot * 512 + (cc + 1) * 128],
                                                                     src_[0][:, c * 128:(c + 1) * 128], ident[:]),
                                         r=[src_[1], B_const], w=[B_PSTR[slot]], sig=(cc == ncg - 1))
                                o_ap = dst_[0][:, c0 * 128:(c0 + ncg) * 128]
                                i_ap = PSTR[:, slot * 512:slot * 512 + ncg * 128]
                                if which == 0:
                                    C.op("act", lambda e: e.copy(o_ap, i_ap), r=[B_PSTR[slot]], w=[dst_[1]])
                                else:
                                    C.op("dve", lambda e: e.tensor_copy(o_ap, i_ap), r=[B_PSTR[slot]], w=[dst_[1]])
                            for cc in range(ncg):
                                c = c0 + cc
                                C.op("pe", lambda e: e.matmul(PSA[:, cc * 128:(cc + 1) * 128], KDT[0][:, c * 128:(c + 1) * 128],
                                                              VT[0][:, c * 128:(c + 1) * 128], start=True, stop=True),
                                     r=[KDT[1], VT[1]], w=[B_PSA], sig=(cc == ncg - 1))
                            C.op("act", lambda e: e.copy(US[dr][0][:, c0 * 128:(c0 + ncg) * 128], PSA[:, 0:ncg * 128]),
                                 r=[B_PSA], w=[US[dr][1]])
                            for cc in range(ncg):
                                c = c0 + cc
                                for I in range(4):
                                    wI = 32 * (I + 1)
                                    offI = (0, 32, 96, 192)[I]
                                    C.op("pe", lambda e: e.matmul(PSB[0:wI, cc * 128 + 32 * I:cc * 128 + 32 * I + 32],
                                                                  KEYS[0][:, c * 320 + offI:c * 320 + offI + wI],
                                                                  QSUB[0][:, c * 128 + 32 * I:c * 128 + 32 * I + 32],
                                                                  start=True, stop=True),
                                         r=[KEYS[1], QSUB[1]], w=[B_PSB], sig=(cc == ncg - 1 and I == 3))
                            for I in range(4):
                                wI = 32 * (I + 1)
                                o_ap = apv(ATT[0][0:wI, :], c0 * 128 + 32 * I, [[128, ncg], [1, 32]])
                                i_ap = apv(PSB[0:wI, :], 32 * I, [[128, ncg], [1, 32]])
                                m_ap = apv(cmask[0:wI, :], 32 * I, [[0, ncg], [1, 32]])
                                C.op("dve", lambda e: e.tensor_tensor(o_ap, i_ap, m_ap, ALU.mult), r=[B_PSB, B_const], w=[ATT[1]])
                            for cc in range(ncg):
                                c = c0 + cc
                                C.op("pe", lambda e: e.matmul(PSA[:, cc * 128:(cc + 1) * 128], VT[0][:, c * 128:(c + 1) * 128],
                                                              ATT[0][:, c * 128:(c + 1) * 128], start=True, stop=True),
                                     r=[VT[1], ATT[1]], w=[B_PSA], sig=(cc == ncg - 1))
                            C.op("act", lambda e: e.copy(OIN[dr][0][:, c0 * 128:(c0 + ncg) * 128], PSA[:, 0:ncg * 128]),
                                 r=[B_PSA], w=[OIN[dr][1]])

                    def recur(dr, c_list, S, save_start=None):
                        first = True
                        for c in c_list:
                            if save_start is not None:
                                if first:
                                    C.op("dve", lambda e: e.memset(save_start[0][:, c * 128:(c + 1) * 128], 0.0), w=[save_start[1]])
                                else:
                                    C.op("act", lambda e: e.copy(save_start[0][:, c * 128:(c + 1) * 128], S[0][:]),
                                         r=[S[1]], w=[save_start[1]])
                            u_ap = US[dr][0][:, c * 128:(c + 1) * 128]
                            if first:
                                C.op("dve", lambda e: e.tensor_copy(S[0][:], u_ap), r=[US[dr][1]], w=[S[1]])
                            else:
                                C.op("dve", lambda e: e.scalar_tensor_tensor(S[0][:], S[0][:], DECS[dr][0][:, c:c + 1], u_ap,
                                                                             ALU.mult, ALU.add),
                                     r=[US[dr][1], DECS[dr][1]], w=[S[1]])
                            first = False

                    GSB = [GS, fbuf("gs2", BF16)]

                    def head_jobs(hd):
                        return [
                            proj(0 * NH + hd, epi_copy(ZR0, "act")),
                            proj(1 * NH + hd, epi_rev(ZS)),
                            proj(2 * NH + hd, epi_copy(VB, "dve")),
                            proj(3 * NH + hd, epi_copy(QF, func=AF.Silu)),
                            proj(4 * NH + hd, epi_copy(GSB[hd % 2], func=AF.Silu)),
                        ]

                    run_jobs(head_jobs(0))
                    for hd in range(NH):
                        GS = GSB[hd % 2]
                        stop_at(5)
                        for dr in range(2):
                            gla_prep(dr, hd)
                            stop_at(6)
                            recur(dr, list(range(NCC)), SCTX[dr], SSTD[dr])
                            recur(dr, list(range(NCC, NCH)), S_, None)
                            C.op("dve", lambda e: e.tensor_copy(XCH[0][:, dr * 128:(dr + 1) * 128], S_[0][:]), r=[S_[1]], w=[XCH[1]])
                        stop_at(7)
                        xi = l * NH + hd
                        Bsrc, Bdst = Buf("xsrc"), Buf("xdst")
                        C.dma("sp", xch_src[xi], XCH[0][:], r=[XCH[1]], w=[Bsrc])
                        C.allgather(allg, xch_src[xi], xch_dst[xi], r=[Bsrc], w=[Bdst])
                        emit_weight()
                        C.dma("sp", apv(XG[0][:], 0, [[XW, GRP], [1, XW]]), xch_dst[xi].rearrange("(r p) n -> p r n", p=128),
                              r=[Bdst], w=[XG[1]])
                        if hd + 1 < NH:
                            run_jobs(head_jobs(hd + 1))
                        for dr in range(2):
                            Sg = SINIT[dr]
                            C.op("dve", lambda e: e.tensor_copy(Sg[0][:], SCTX[dr][0][:]), r=[SCTX[dr][1]], w=[Sg[1]])
                            order = range(GRP) if dr == 0 else range(GRP - 1, -1, -1)
                            for i in order:
                                Sl = XG[0][:, i * XW + dr * 128:i * XW + (dr + 1) * 128]
                                Dl = XG[0][:, i * XW + 256 + dr:i * XW + 257 + dr]
                                mk = SEL[:, dr * 4 + i:dr * 4 + i + 1]
                                C.op("dve", lambda e: e.scalar_tensor_tensor(S2[0][:], Sg[0][:], Dl, Sl, ALU.mult, ALU.add),
                                     r=[Sg[1], XG[1]], w=[S2[1]])
                                C.op("dve", lambda e: e.tensor_tensor(S2[0][:], S2[0][:], Sg[0][:], ALU.subtract), r=[Sg[1]], w=[S2[1]])
                                C.op("dve", lambda e: e.scalar_tensor_tensor(Sg[0][:], S2[0][:], mk, Sg[0][:], ALU.mult, ALU.add),
                                     r=[S2[1], B_const], w=[Sg[1]])
                            for c in range(NCC, NCH):
                                C.op("act", lambda e: e.copy(SSTD[dr][0][:, c * 128:(c + 1) * 128], Sg[0][:]),
                                     r=[Sg[1]], w=[SSTD[dr][1]])
                                if c < NCH - 1:
                                    C.op("dve", lambda e: e.scalar_tensor_tensor(Sg[0][:], Sg[0][:], DECS[dr][0][:, c:c + 1],
                                                                                 US[dr][0][:, c * 128:(c + 1) * 128], ALU.mult, ALU.add),
                                         r=[US[dr][1], DECS[dr][1]], w=[Sg[1]])
                            for c0 in range(0, NCH, 4):
                                ncg = min(4, NCH - c0)
                                for cc in range(ncg):
                                    c = c0 + cc
                                    C.op("pe", lambda e: e.matmul(PSA[:, cc * 128:(cc + 1) * 128], SSTD[dr][0][:, c * 128:(c + 1) * 128],
                                                                  QEK[dr][0][:, c * 128:(c + 1) * 128], start=True, stop=True),
                                         r=[SSTD[dr][1], QEK[dr][1]], w=[B_PSA], sig=(cc == ncg - 1))
                                o_ap = OIN[dr][0][:, c0 * 128:(c0 + ncg) * 128]
                                C.op("dve", lambda e: e.tensor_tensor(o_ap, o_ap, PSA[:, 0:ncg * 128], ALU.add),
                                     r=[B_PSA], w=[OIN[dr][1]])
                        OSUM = OIN[0]
                        for (a, b) in SEGS:
                            rv = apv(OIN[1][0][:], b - 1, [[-1, b - a]])
                            C.op("dve", lambda e: e.tensor_tensor(OSUM[0][:, a:b], OSUM[0][:, a:b], rv, ALU.add), r=[OIN[1][1]], w=[OSUM[1]])
                        stop_at(8)
                        C.op("act", lambda e: e.activation(W1[0][:], OSUM[0][:], AF.Square), r=[OSUM[1]], w=[W1[1]])
                        for (t0, tn) in TT:
                            C.op("pe", lambda e: e.matmul(PSS[:, 0:tn], ones_f[:], W1[0][:, t0:t0 + tn], start=True, stop=True),
                                 r=[W1[1], B_const], w=[B_PSS])
                            C.op("dve", lambda e: e.tensor_scalar(W2[0][:, t0:t0 + tn], PSS[:, 0:tn], 1.0 / 128.0, RMS_EPS, ALU.mult, ALU.add),
                                 r=[B_PSS], w=[W2[1]])
                        C.op("act", lambda e: e.activation(W2[0][:], W2[0][:], AF.Sqrt), w=[W2[1]])
                        C.op("dve", lambda e: e.reciprocal(W2[0][:], W2[0][:]), w=[W2[1]])
                        C.op("dve", lambda e: e.tensor_tensor(W2[0][:], W2[0][:], OSUM[0][:], ALU.mult), r=[OSUM[1]], w=[W2[1]])
                        C.op("dve", lambda e: e.scalar_tensor_tensor(OAB[0][:], W2[0][:], GN[:, l:l + 1], GS[0][:], ALU.mult, ALU.mult),
                             r=[W2[1], GS[1], B_const], w=[OAB[1]])
                        C.dma("sp", oa_d[:, hd * NT:(hd + 1) * NT], OAB[0][:], r=[OAB[1]], w=[B_oa])

                    stop_at(9)
                    nrow = NL // cfg.GRID_W
                    for cb in range(NH):
                        jobs = [proj(5 * NH + j * NH + cb, epi_copy(P3[j], "act" if j != 1 else "dve")) for j in range(3)]
                        run_jobs(jobs)
                        Bg, Cg, Hh = P3
                        Z, Y = W1, W2
                        C.op("dve", lambda e: e.tensor_tensor(Z[0][:], Cg[0][:], Hh[0][:], ALU.mult), r=[Cg[1], Hh[1]], w=[Z[1]])
                        cw = lambda j: CW[:, (l * NH + cb) * 3 + j:(l * NH + cb) * 3 + j + 1]
                        C.op("dve", lambda e: e.tensor_scalar(Y[0][:], Z[0][:], cw(1), None, ALU.mult), r=[Z[1], B_const], w=[Y[1]])
                        for (a, n_r, rl) in ((0, 1, CTX), (CTX, nrow, cfg.GRID_W)):
                            ya = apv(Y[0][:], a + 1, [[rl, n_r], [1, rl - 1]])
                            za = apv(Z[0][:], a, [[rl, n_r], [1, rl - 1]])
                            C.op("dve", lambda e: e.scalar_tensor_tensor(ya, za, cw(0), ya, ALU.mult, ALU.add), r=[Z[1]], w=[Y[1]])
                            yb = apv(Y[0][:], a, [[rl, n_r], [1, rl - 1]])
                            zb_ = apv(Z[0][:], a + 1, [[rl, n_r], [1, rl - 1]])
                            C.op("dve", lambda e: e.scalar_tensor_tensor(yb, zb_, cw(2), yb, ALU.mult, ALU.add), r=[Z[1]], w=[Y[1]])
                        C.op("dve", lambda e: e.tensor_tensor(OAB[0][:], Bg[0][:], Y[0][:], ALU.mult), r=[Bg[1], Y[1]], w=[OAB[1]])
                        C.dma("sp", zb_d[:, cb * NT:(cb + 1) * NT], OAB[0][:], r=[OAB[1]], w=[B_zb])
                    C.barrier()

                with ExitStack() as ph:
                    def tb(name, w, dt=F32):
                        return sb(ph, name, [128, w], dt), Buf(name)
                    alloc_slabs(ph, 3)
                    HC2 = DFFC // 2
                    XT = tb("xt", KC * T)
                    UT = tb("ut", KC * T, BF16)
                    HT = tb("ht", HC2 * T, BF16)
                    YT = (HT[0][:, 0:KC * T], HT[1])
                    OAT = (HT[0][:, KC * T:(KC + NH) * T], HT[1])
                    ZBT = (HT[0][:, (KC + NH) * T:(KC + 2 * NH) * T], HT[1])
                    R0, R1, R2, R3 = tb("r0", T), tb("r1", T), tb("r2", T), tb("r3", T)
                    SA, SB_ = R0, R1
                    TM2 = [R0, R1]
                    TMP = [R0, R1]
                    MEAN, RSTD = R2, R3
                    cnt = [0]

                    def acc_slot():
                        a = cnt[0] % 4
                        cnt[0] += 1
                        return ACC[a], B_ACC[a][0]

                    def mm(ps, bps, pairs, rbufs, start=True, stop=True):
                        n = len(pairs)
                        for i, (l_, r_) in enumerate(pairs):
                            C.op("pe", lambda e: e.matmul(ps, l_, r_, start=(start and i == 0), stop=(stop and i == n - 1)),
                                 r=rbufs, w=[bps], sig=(i == n - 1))

                    def layer_norm(which, Tt, dst_fn):
                        for kc in range(KC):
                            i = kc % 2
                            v_ap = XT[0][:, kc * T:kc * T + Tt]
                            C.op("act", lambda e: e.activation(TMP[i][0][:, 0:Tt], v_ap, AF.Square), r=[XT[1]], w=[TMP[i][1]])
                            C.op("pe", lambda e: e.matmul(PSS[:, 0:Tt], ones_f[:], v_ap, start=(kc == 0), stop=(kc == KC - 1)),
                                 r=[XT[1], B_const], w=[B_PSS], sig=False)
                            C.op("pe", lambda e: e.matmul(PSA[:, 0:Tt], ones_f[:], TMP[i][0][:, 0:Tt], start=(kc == 0), stop=(kc == KC - 1)),
                                 r=[TMP[i][1]], w=[B_PSA], sig=True)
                        mean, rstd = MEAN[0][:, 0:Tt], RSTD[0][:, 0:Tt]
                        C.op("dve", lambda e: e.tensor_scalar(mean, PSS[:, 0:Tt], 1.0 / D, None, ALU.mult), r=[B_PSS], w=[MEAN[1]])
                        C.op("dve", lambda e: e.tensor_tensor(rstd, mean, mean, ALU.mult), r=[MEAN[1]], w=[RSTD[1]])
                        C.op("dve", lambda e: e.scalar_tensor_tensor(rstd, PSA[:, 0:Tt], 1.0 / D, rstd, ALU.mult, ALU.subtract),
                             r=[B_PSA], w=[RSTD[1]])
                        C.op("dve", lambda e: e.tensor_scalar(rstd, rstd, LN_EPS, None, ALU.add), w=[RSTD[1]])
                        C.op("act", lambda e: e.activation(rstd, rstd, AF.Sqrt), w=[RSTD[1]])
                        C.op("dve", lambda e: e.reciprocal(rstd, rstd), w=[RSTD[1]])
                        for kc in range(KC):
                            v_ap = XT[0][:, kc * T:kc * T + Tt]
                            gi = (l * 2 + which) * KC + kc
                            C.op("dve", lambda e: e.tensor_tensor(v_ap, v_ap, mean, ALU.subtract), r=[MEAN[1]], w=[XT[1]])
                            C.op("dve", lambda e: e.tensor_tensor(v_ap, v_ap, rstd, ALU.mult), r=[RSTD[1]], w=[XT[1]])
                            C.op("act", lambda e: e.activation(v_ap, v_ap, AF.Identity, bias=LNB[:, gi:gi + 1], scale=LNG[:, gi:gi + 1]),
                                 r=[B_const], w=[XT[1]])
                            dst_fn(kc)

                    tiles = [(t0, min(T, CTX - t0)) for t0 in range(0, CTX, T)] + [(t0, min(T, NT - t0)) for t0 in range(CTX, NT, T)]
                    for (t0, Tt) in tiles:
                        is_ctx = t0 < CTX
                        jj = 1 if is_ctx else 0
                        if last and is_ctx:
                            continue
                        C.dma("sp", apv(XT[0][:], 0, [[T, KC], [1, Tt]]), apv(x_src, t0, [[NT, KC], [1, Tt]]), r=[B_xs], w=[XT[1]])
                        C.dma("sp", apv(OAT[0], 0, [[T, NH], [1, Tt]]), apv(oa_d, t0, [[NT, NH], [1, Tt]]), r=[B_oa], w=[OAT[1]])
                        C.dma("sp", apv(ZBT[0], 0, [[T, NH], [1, Tt]]), apv(zb_d, t0, [[NT, NH], [1, Tt]]), r=[B_zb], w=[ZBT[1]])
                        for kc in range(KC):
                            C.op("dve", lambda e: e.tensor_scalar(UT[0][:, kc * T:kc * T + Tt], XT[0][:, kc * T:kc * T + Tt],
                                                                  modcol(l, jj, 1, kc), modcol(l, jj, 0, kc), ALU.mult, ALU.add),
                                 r=[XT[1], B_MOD], w=[UT[1]])
                        sa, sb_ = SA[0][:, 0:Tt], SB_[0][:, 0:Tt]
                        jobs = []
                        for fc in range(KC):
                            g0 = (8 * NH + fc) * 128
                            g1 = (8 * NH + KC + fc) * 128
                            specs = ((Win[g0:g0 + 128, :], Bw["w_in"], KC, UT, 0), (Win[g1:g1 + 128, :], Bw["w_in"], KC, UT, 1),
                                     (Wa_[fc * 128:(fc + 1) * 128, :], Bw["w_out_a"], NH, OAT, 2),
                                     (Wb_[fc * 128:(fc + 1) * 128, :], Bw["w_out_b"], NH, ZBT, 3))
                            for (wsrc, wb_, nk, rhs, kind) in specs:
                                def ld(wsrc=wsrc, wb_=wb_, nk=nk):
                                    return load_slab(wsrc, wb_, nk * 128)

                                def cp(res, nk=nk, rhs=rhs, kind=kind, fc=fc):
                                    slab, bsl = res
                                    acc, bps = acc_slot()
                                    ps = acc[:, 0:Tt]
                                    rap = rhs[0] if isinstance(rhs[0], bass.AP) else rhs[0][:]
                                    mm(ps, bps, [(slab[:, k * 128:(k + 1) * 128], apv(rap, k * T, [[1, Tt]])) for k in range(nk)], [bsl, rhs[1]])
                                    if kind == 0:
                                        C.op("act", lambda e: e.activation(sa, ps, AF.Sigmoid), r=[bps], w=[SA[1]])
                                    elif kind == 1:
                                        C.op("act", lambda e: e.activation(sb_, ps, AF.Sigmoid), r=[bps], w=[SB_[1]])
                                    elif kind == 2:
                                        C.op("dve", lambda e: e.tensor_tensor(sa, sa, ps, ALU.mult), r=[bps], w=[SA[1]])
                                    else:
                                        C.op("dve", lambda e: e.tensor_tensor(sb_, sb_, ps, ALU.mult), r=[bps], w=[SB_[1]])
                                        C.op("dve", lambda e: e.tensor_tensor(apv(YT[0], fc * T, [[1, Tt]]), sa, sb_, ALU.add),
                                             r=[SA[1], SB_[1]], w=[YT[1]])
                                jobs.append((ld, cp))
                        run_jobs(jobs, depth=2)
                        jobs = []
                        for fc in range(KC):
                            def ld(fc=fc):
                                return load_slab(Wo_[fc * 128:(fc + 1) * 128, :], Bw["w_o"], KC * 128)

                            def cp(res, fc=fc):
                                slab, bsl = res
                                i = fc % 2
                                acc, bps = acc_slot()
                                ps = acc[:, 0:Tt]
                                mm(ps, bps, [(slab[:, k * 128:(k + 1) * 128], apv(YT[0], k * T, [[1, Tt]])) for k in range(KC)], [bsl, YT[1]])
                                tm = TM2[i][0][:, 0:Tt]
                                C.op("act", lambda e: e.activation(tm, ps, AF.Copy, scale=modcol(l, jj, 2, fc)),
                                     r=[bps, B_MOD], w=[TM2[i][1]])
                                x_ap = XT[0][:, fc * T:fc * T + Tt]
                                C.op("dve", lambda e: e.scalar_tensor_tensor(x_ap, x_ap, ALPHA, tm, ALU.mult, ALU.add),
                                     r=[TM2[i][1]], w=[XT[1]])
                            jobs.append((ld, cp))
                        run_jobs(jobs, depth=2)

                        def mk_u2(kc):
                            C.op("dve", lambda e: e.tensor_scalar(UT[0][:, kc * T:kc * T + Tt], XT[0][:, kc * T:kc * T + Tt],
                                                                  modcol(l, jj, 4, kc), modcol(l, jj, 3, kc), ALU.mult, ALU.add),
                                 r=[XT[1], B_MOD], w=[UT[1]])
                        layer_norm(0, Tt, mk_u2)
                        jobs = []
                        NQH = HC2 // KC
                        for hf in range(2):
                            for hh in range(HC2):
                                hc = hf * HC2 + hh

                                def ld(hc=hc):
                                    return load_slab(Wup[hc * 128:(hc + 1) * 128, :], Bw["w_up"], KC * 128)

                                def cp(res, hh=hh):
                                    slab, bsl = res
                                    i = hh % 2
                                    acc, bps = acc_slot()
                                    ps = acc[:, 0:Tt]
                                    mm(ps, bps, [(slab[:, k * 128:(k + 1) * 128], UT[0][:, k * T:k * T + Tt]) for k in range(KC)], [bsl, UT[1]])
                                    tm = TM2[i][0][:, 0:Tt]
                                    C.op("act", lambda e: e.activation(tm, ps, AF.Relu), r=[bps], w=[TM2[i][1]])
                                    C.op("dve", lambda e: e.tensor_tensor(HT[0][:, hh * T:hh * T + Tt], tm, tm, ALU.mult),
                                         r=[TM2[i][1]], w=[HT[1]])
                                jobs.append((ld, cp))
                            for fc in range(KC):
                                state = {}
                                for q in range(NQH):
                                    qg = hf * NQH + q

                                    def ld(fc=fc, qg=qg):
                                        return load_slab(Wdn[fc * 128:(fc + 1) * 128, qg * KC * 128:(qg + 1) * KC * 128], Bw["w_down"], KC * 128)

                                    def cp(res, fc=fc, q=q, hf=hf, state=state):
                                        slab, bsl = res
                                        if q == 0:
                                            state["acc"] = acc_slot()
                                        acc, bps = state["acc"]
                                        ps = acc[:, 0:Tt]
                                        mm(ps, bps, [(slab[:, k * 128:(k + 1) * 128], HT[0][:, (q * KC + k) * T:(q * KC + k) * T + Tt]) for k in range(KC)],
                                           [bsl, HT[1]], start=(q == 0), stop=(q == NQH - 1))
                                        if q == NQH - 1:
                                            i = fc % 2
                                            tm = TM2[i][0][:, 0:Tt]
                                            C.op("act", lambda e: e.activation(tm, ps, AF.Copy, scale=modcol(l, jj, 5, fc)),
                                                 r=[bps, B_MOD], w=[TM2[i][1]])
                                            x_ap = XT[0][:, fc * T:fc * T + Tt]
                                            if hf == 0:
                                                C.op("dve", lambda e: e.scalar_tensor_tensor(x_ap, x_ap, ALPHA, tm, ALU.mult, ALU.add),
                                                     r=[TM2[i][1]], w=[XT[1]])
                                            else:
                                                C.op("dve", lambda e: e.tensor_tensor(x_ap, x_ap, tm, ALU.add), r=[TM2[i][1]], w=[XT[1]])
                                    jobs.append((ld, cp))
                        run_jobs(jobs, depth=2)
                        layer_norm(1, Tt, lambda kc: None)
                        if not last:
                            C.dma("sp", apv(xs_d, t0, [[NT, KC], [1, Tt]]), apv(XT[0][:], 0, [[T, KC], [1, Tt]]), r=[XT[1]], w=[B_xs])
                        else:
                            C.dma("sp", apv(outT, t0 - CTX, [[NL, KC], [1, Tt]]), apv(XT[0][:], 0, [[T, KC], [1, Tt]]), r=[XT[1]], w=[Buf("out")])
                    C.barrier()
        except _Stop:
            pass
        STOPPED[0] = False
        C.barrier()
    return nc


def fm(a, ):
    F, n = a.shape
    return np.ascontiguousarray(a.reshape(F // 128, 128, n).transpose(1, 0, 2).reshape(128, (F // 128) * n))


def wblock(W):
    K, N = W.shape
    return np.ascontiguousarray(W.reshape(K // 128, 128, N // 128, 128).transpose(2, 1, 0, 3).reshape(N, K))


def prepare_inputs(cfg, x, c, ctx, c_ctx, w_ada, b_ada, w_in, conv_w, gnorm_g, w_out_a, w_out_b, w_o,
                   w_up, w_down, ln_g, ln_b, lower_bounds):
    f32 = np.float32
    D, KC, NH, DEPTH, NL, CTX = cfg.D, cfg.KC, cfg.NH, cfg.DEPTH, cfg.NL, cfg.CTX
    OCA = 6 * KC // GRP
    ws = {"w_in": w_in, "w_out_a": w_out_a, "w_out_b": w_out_b, "w_o": w_o, "w_up": w_up, "w_down": w_down}
    shared = {}
    wblk = {}
    for l in range(DEPTH):
        for nm in WNAMES:
            wblk[(l, nm)] = wblock(np.asarray(ws[nm][l], f32))
    wada_blk = [wblock(np.asarray(w_ada[l], f32)) for l in range(DEPTH)]
    cw = np.asarray(conv_w, f32)
    shared["conv_w"] = np.ascontiguousarray(
        cw.reshape(DEPTH, 3, NH, 128).transpose(3, 0, 2, 1).reshape(128, DEPTH * NH * 3))
    shared["gnorm_g"] = np.ascontiguousarray(np.asarray(gnorm_g, f32).T)
    shared["ln_g"] = np.ascontiguousarray(np.asarray(ln_g, f32).reshape(DEPTH, 2, KC, 128).transpose(3, 0, 1, 2).reshape(128, -1))
    shared["ln_b"] = np.ascontiguousarray(np.asarray(ln_b, f32).reshape(DEPTH, 2, KC, 128).transpose(3, 0, 1, 2).reshape(128, -1))
    shared["lbraw"] = np.ascontiguousarray(
        np.asarray(lower_bounds, f32).reshape(DEPTH, 2, NH, 128).transpose(3, 0, 1, 2).reshape(128, -1))
    in_maps = []
    for r in range(NCORES):
        b, j = r // GRP, r % GRP
        m = dict(shared)
        xt = np.concatenate([np.asarray(ctx[b], f32).T, np.asarray(x[b, j * NL:(j + 1) * NL], f32).T], axis=1)
        m["xT"] = fm(xt)
        m["cvec"] = fm(np.stack([np.asarray(c[b], f32), np.asarray(c_ctx, f32)], axis=1))
        s = np.zeros((128, 8), f32)
        for i in range(GRP):
            s[:, i] = 1.0 if i < j else 0.0
            s[:, 4 + i] = 1.0 if i > j else 0.0
        m["sel"] = s
        for l in range(DEPTH):
            for nm in WNAMES:
                blk = wblk[(l, nm)]
                m[f"{nm}_{l}"] = blk
            m[f"w_ada_{l}"] = wada_blk[l][j * OCA * 128:(j + 1) * OCA * 128]
        ba = np.asarray(b_ada, f32).reshape(DEPTH, GRP, OCA, 128)[:, j]
        m["b_ada"] = np.ascontiguousarray(ba.transpose(2, 0, 1).reshape(128, DEPTH * OCA))
        in_maps.append(m)
    return in_maps


def assemble_output(cfg, results):
    D, KC, NL = cfg.D, cfg.KC, cfg.NL
    out = np.zeros((cfg.BATCH, cfg.SEQ, D), np.float32)
    for r in range(NCORES):
        b, j = r // GRP, r % GRP
        o = np.asarray(results[r]["outT"]).reshape(128, KC, NL).transpose(2, 1, 0).reshape(NL, D)
        out[b, j * NL:(j + 1) * NL] = o
    return out


def run(cfg, **inputs):
    nc = build_program(cfg)
    in_maps = prepare_inputs(cfg, **inputs)
    res = run_bass_kernel_spmd(nc, in_maps, core_ids=list(range(NCORES)))
    return assemble_output(cfg, res.results)


def kernel(**inputs):
    return run(Cfg(), **inputs)
```

```python
from contextlib import ExitStack

import numpy as np
import concourse.bass as bass
import concourse.mybir as mybir
from concourse.bass_utils import run_bass_kernel_spmd

F32 = mybir.dt.float32
BF16 = mybir.dt.bfloat16
AF = mybir.ActivationFunctionType
ALU = mybir.AluOpType
AX = mybir.AxisListType

LN_EPS = 1e-5
RMS_EPS = 1e-6
NCORES = 8
GRP = 4


class _Stop(Exception):
    pass


STOP = [None]


STOPPED = [False]


def stop_at(n):
    if STOP[0] == n:
        STOPPED[0] = True


class Cfg:
    def __init__(self, D=4096, SEQ=4096, CTX=256, DEPTH=2, T=512):
        self.D, self.SEQ, self.CTX, self.DEPTH, self.T = D, SEQ, CTX, DEPTH, T
        self.BATCH = 2
        self.GRID_W = 64
        self.NH = D // 256
        self.WA = self.NH * 128
        self.WB = D // 2
        self.DFF = 4 * D
        self.KC = D // 128
        self.NL = SEQ // GRP
        self.NT = CTX + self.NL
        self.NIN = 5 * self.WA + 3 * self.WB + 2 * D
        self.NMOD = 6 * D
        self.ALPHA = (2.0 * DEPTH) ** 0.25
        self.NCH = self.NT // 128
        assert self.NL % 128 == 0 and CTX % 128 == 0 and T % 128 == 0 and T <= 512
        assert (self.NIN // 128) % GRP == 0 and (6 * self.KC) % GRP == 0 and self.KC % GRP == 0


class Buf:
    __slots__ = ("name", "w", "r")

    def __init__(self, name):
        self.name, self.w, self.r = name, None, {}

    def set_write(self, h):
        self.w, self.r = h, {}

    def add_read(self, h):
        k = id(h[0])
        if k not in self.r or self.r[k][1] < h[1]:
            self.r[k] = h


class Ctx:
    ENG = ("pe", "act", "dve", "pool", "sp")

    def __init__(self, nc, stack):
        self.nc, self.stack = nc, stack
        self.eng = {"pe": nc.tensor, "act": nc.scalar, "dve": nc.vector, "pool": nc.gpsimd, "sp": nc.sync}
        self.nsem = 0
        self.sem = {k: self._newsem() for k in self.ENG}
        self.cnt = {k: 0 for k in self.ENG}
        self.seen = {k: {} for k in self.ENG}
        self.pending = {k: [] for k in self.ENG}
        self.dpool = {k: [[self._newsem(), 0] for _ in range(8)] for k in ("sp", "pool")}
        self.dnext = {"sp": 0, "pool": 0}
        self.ccsem = [self._newsem(), 0]

    def _newsem(self):
        self.nsem += 1
        return self.stack.enter_context(self.nc.semaphore(f"sm{self.nsem}"))

    def _wait(self, k, deps):
        e = self.eng[k]
        for d in deps:
            if d is None:
                continue
            sem, n = d
            key = id(sem)
            if self.seen[k].get(key, 0) >= n:
                continue
            e.wait_ge(sem, n)
            self.seen[k][key] = n

    @staticmethod
    def _deps(r, w, deps):
        out = list(deps)
        for b in r:
            if b.w is not None:
                out.append(b.w)
        for b in w:
            if b.w is not None:
                out.append(b.w)
            out.extend(b.r.values())
        return out

    def op(self, k, fn, r=(), w=(), deps=(), sig=True):
        if STOPPED[0]:
            return None
        self._wait(k, self._deps(r, w, deps))
        ins = fn(self.eng[k])
        if not sig:
            self.pending[k].append((tuple(r), tuple(w)))
            return None
        if self.cnt[k] >= 30000:
            self.sem[k] = self._newsem()
            self.cnt[k] = 0
        ins.then_inc(self.sem[k], 1)
        self.cnt[k] += 1
        h = (self.sem[k], self.cnt[k])
        for (pr, pw) in self.pending[k]:
            for b in pr:
                b.add_read(h)
            for b in pw:
                b.set_write(h)
        self.pending[k] = []
        for b in r:
            b.add_read(h)
        for b in w:
            b.set_write(h)
        return h

    def dma(self, k, out, in_, r=(), w=(), deps=()):
        if STOPPED[0]:
            return None
        slot = self.dpool[k][self.dnext[k]]
        self.dnext[k] = (self.dnext[k] + 1) % len(self.dpool[k])
        if slot[1] >= 30000:
            slot[0], slot[1] = self._newsem(), 0
        prev = (slot[0], slot[1]) if slot[1] > 0 else None
        self._wait(k, self._deps(r, w, list(deps) + [prev]))
        self.eng[k].dma_start(out=out, in_=in_).then_inc(slot[0], 16)
        slot[1] += 16
        h = (slot[0], slot[1])
        for b in r:
            b.add_read(h)
        for b in w:
            b.set_write(h)
        return h

    def allgather(self, groups, src, dst, r=(), w=()):
        if STOPPED[0]:
            return None
        self._wait("pool", self._deps(r, w, ()))
        self.nc.gpsimd.collective_compute("AllGather", ALU.bypass, replica_groups=groups,
                                          ins=[src.opt()], outs=[dst.opt()]).then_inc(self.ccsem[0])
        self.ccsem[1] += 1
        h = (self.ccsem[0], self.ccsem[1])
        for b in r:
            b.add_read(h)
        for b in w:
            b.set_write(h)
        return h

    def barrier(self, bufs=()):
        if STOPPED[0]:
            return
        hs = [(self.sem[k], self.cnt[k]) for k in self.ENG if self.cnt[k] > 0]
        for k in ("sp", "pool"):
            hs += [(s[0], s[1]) for s in self.dpool[k] if s[1] > 0]
        if self.ccsem[1] > 0:
            hs.append((self.ccsem[0], self.ccsem[1]))
        for k in ("pe", "act", "dve", "sp", "pool"):
            self._wait(k, hs)


def apv(ap, off, dims):
    return bass.AP(tensor=ap.tensor, offset=ap.offset + off, ap=[list(ap.ap[0])] + [list(d) for d in dims])


WNAMES = ("w_in", "w_out_a", "w_out_b", "w_o", "w_up", "w_down")


def wshape(cfg, name):
    return {"w_in": (cfg.D, cfg.NIN), "w_out_a": (cfg.WA, cfg.D), "w_out_b": (cfg.WB, cfg.D),
            "w_o": (cfg.D, cfg.D), "w_up": (cfg.D, cfg.DFF), "w_down": (cfg.DFF, cfg.D)}[name]


def build_program(cfg):
    STOPPED[0] = False
    nc = bass.Bass("TRN2", target_bir_lowering=False)
    D, KC, NT, NL, CTX, NH, T, NCH, DEPTH = cfg.D, cfg.KC, cfg.NT, cfg.NL, cfg.CTX, cfg.NH, cfg.T, cfg.NCH, cfg.DEPTH
    DFFC = cfg.DFF // 128
    OCA = 6 * KC // GRP
    ALPHA = float(cfg.ALPHA)
    NCC = CTX // 128
    NLC = NL // 128
    SEGS = ((0, CTX), (CTX, NT))

    def din(name, shape, dt=F32):
        return nc.dram_tensor(name, list(shape), dt, kind="ExternalInput").ap()

    xT = din("xT", [128, KC * NT])
    cvec = din("cvec", [128, KC * 2])
    sel = din("sel", [128, 8])
    wsh = {}
    for l in range(DEPTH):
        for nm in WNAMES:
            K, N = wshape(cfg, nm)
            wsh[(l, nm)] = din(f"{nm}_{l}", [N, K])
    wada = [din(f"w_ada_{l}", [OCA * 128, D]) for l in range(DEPTH)]
    bada = din("b_ada", [128, DEPTH * OCA])
    convw = din("conv_w", [128, DEPTH * NH * 3])
    gng = din("gnorm_g", [128, DEPTH])
    lng = din("ln_g", [128, DEPTH * 2 * KC])
    lnb = din("ln_b", [128, DEPTH * 2 * KC])
    lbr = din("lbraw", [128, DEPTH * 2 * NH])
    outT = nc.dram_tensor("outT", [128, KC * NL], F32, kind="ExternalOutput").ap()

    wfull = wsh
    mod_src = nc.dram_tensor("mod_src", [128, DEPTH * OCA * 2], F32).ap()
    mod_all = nc.dram_tensor("mod_all", [GRP * 128, DEPTH * OCA * 2], F32).ap()
    xs_d = nc.dram_tensor("xs_d", [128, KC * NT], F32).ap()
    oa_d = nc.dram_tensor("oa_d", [128, NH * NT], BF16).ap()
    zb_d = nc.dram_tensor("zb_d", [128, NH * NT], BF16).ap()
    XW = 2 * 128 + 2
    xch_src = [nc.dram_tensor(f"xch_src{i}", [128, XW], F32).ap() for i in range(DEPTH * NH)]
    xch_dst = [nc.dram_tensor(f"xch_dst{i}", [GRP * 128, XW], F32).ap() for i in range(DEPTH * NH)]

    B_xs, B_oa, B_zb = Buf("xs_d"), Buf("oa_d"), Buf("zb_d")
    B_wfull = {k: Buf(f"wfull{k}") for k in wfull}

    with ExitStack() as st:
        C = Ctx(nc, st)

        uniq = [0]

        def sb(stack, name, shape, dt):
            uniq[0] += 1
            return stack.enter_context(nc.sbuf_tensor(f"{name}_{uniq[0]}", list(shape), dt))

        def pst(name, shape, dt):
            return st.enter_context(nc.psum_tensor(name, list(shape), dt))

        ACC = [pst(f"acc{i}", [128, 512], F32) for i in range(4)]
        B_ACC = [[Buf(f"acc{i}")] for i in range(4)]
        PSA = pst("psA", [128, 512], F32)
        PSB = pst("psB", [128, 512], F32)
        PSTR = pst("psT", [128, 1024], BF16)
        PSS = pst("psS", [128, 512], F32)
        B_PSA, B_PSB, B_PSS = Buf("psA"), Buf("psB"), Buf("psS")
        _bpt = Buf("psT")
        B_PSTR = [_bpt, _bpt]

        ident_f = sb(st, "ident_f", [128, 128], F32)
        ident = sb(st, "ident", [128, 128], BF16)
        ones_f = sb(st, "ones_f", [128, 128], F32)
        cmask = sb(st, "cmask", [128, 128], F32)
        mk128 = sb(st, "mk128", [128, 128], F32)
        mk32 = sb(st, "mk32", [128, 32], F32)
        MOD = sb(st, "MOD", [128, DEPTH * 2 * 6 * KC], F32)
        LB = sb(st, "LB", [128, DEPTH * 2 * NH], F32)
        OML = sb(st, "OML", [128, DEPTH * 2 * NH], F32)
        CW = sb(st, "CW", [128, DEPTH * NH * 3], F32)
        GN = sb(st, "GN", [128, DEPTH], F32)
        LNG = sb(st, "LNG", [128, DEPTH * 2 * KC], F32)
        LNB = sb(st, "LNB", [128, DEPTH * 2 * KC], F32)
        SEL = sb(st, "SEL", [128, 8], F32)
        zeros_b = sb(st, "zeros_b", [128, 128], BF16)
        B_const = Buf("const")
        B_MOD = Buf("MOD")

        def modcol(l, j, part, c):
            i = ((l * 2 + j) * 6 + part) * KC + c
            return MOD[:, i:i + 1]

        SLABW = KC * 128
        slabs, B_slab = [], []
        slab_i = [0]

        def alloc_slabs(stack, n):
            slabs[:] = [sb(stack, f"slab{i}", [128, SLABW], BF16) for i in range(n)]
            B_slab[:] = [Buf(f"slab{i}") for i in range(n)]
            slab_i[0] = 0

        def load_slab(src2d, bsrc, n):
            i = slab_i[0]
            slab_i[0] = (i + 1) % len(slabs)
            C.dma("pool", slabs[i][:, 0:n], src2d, r=[bsrc], w=[B_slab[i]])
            return slabs[i], B_slab[i]

        def run_jobs(jobs, depth=1):
            n = len(jobs)
            loaded = [jobs[i][0]() for i in range(min(depth, n))]
            for i in range(n):
                jobs[i][1](loaded[i])
                loaded[i] = None
                if i + depth < n:
                    loaded.append(jobs[i + depth][0]())

        def mm_acc(ps_ap, bps, pairs, rbufs):
            n = len(pairs)
            h = None
            for i, (l_, r_) in enumerate(pairs):
                h = C.op("pe", lambda e: e.matmul(ps_ap, l_, r_, start=(i == 0), stop=(i == n - 1)),
                         r=rbufs, w=[bps], sig=(i == n - 1))
            return h

        try:
            C.op("pool", lambda e: e.memset(ident_f[:], 0.0), w=[B_const])
            C.op("pool", lambda e: e.affine_select(ident_f[:], ident_f[:], pattern=[[-1, 128]], compare_op=ALU.not_equal,
                                                   fill=1.0, base=0, channel_multiplier=1), w=[B_const])
            C.op("pool", lambda e: e.memset(ones_f[:], 1.0), w=[B_const])
            C.op("pool", lambda e: e.memset(cmask[:], 1.0), w=[B_const])
            C.op("pool", lambda e: e.affine_select(cmask[:], cmask[:], pattern=[[1, 128]], compare_op=ALU.is_ge,
                                                   fill=0.0, base=0, channel_multiplier=-1), w=[B_const])
            C.op("pool", lambda e: e.memset(mk128[:], 1.0), w=[B_const])
            C.op("pool", lambda e: e.memset(mk32[:], 1.0), w=[B_const])
            C.op("pool", lambda e: e.memset(mk128[:, 0:1], 0.0), w=[B_const])
            C.op("pool", lambda e: e.memset(mk32[:, 0:1], 0.0), w=[B_const])
            C.op("pool", lambda e: e.memset(zeros_b[:], 0.0), w=[B_const])
            C.op("dve", lambda e: e.tensor_copy(ident[:], ident_f[:]), r=[B_const], w=[B_const])
            for (dst_, src_) in ((CW, convw), (GN, gng), (LNG, lng), (LNB, lnb), (SEL, sel), (LB, lbr)):
                C.dma("sp", dst_[:], src_, w=[B_const])

            allg = [[0, 1, 2, 3], [4, 5, 6, 7]]
            wqueue = []

            def emit_weight():
                return

            def ensure_weights(l):
                while wqueue and wqueue[0][0] <= l:
                    emit_weight()

            with ExitStack() as ph:
                W = 2 * NH
                ex = sb(ph, "lb_ex", [128, DEPTH * W], F32)
                den = sb(ph, "lb_den", [128, W], F32)
                Bt = Buf("lbtmp")
                C.op("act", lambda e: e.activation(ex[:], LB[:], AF.Exp), r=[B_const], w=[Bt])
                C.op("dve", lambda e: e.tensor_copy(den[:], ex[:, 0:W]), r=[Bt], w=[Bt])
                for l in range(1, DEPTH):
                    C.op("dve", lambda e: e.tensor_tensor(den[:], den[:], ex[:, l * W:(l + 1) * W], ALU.add), r=[Bt], w=[Bt])
                C.op("dve", lambda e: e.reciprocal(den[:], den[:]), r=[Bt], w=[Bt])
                C.op("dve", lambda e: e.memset(LB[:, 0:W], 0.0), r=[Bt], w=[B_const])
                for l in range(1, DEPTH):
                    C.op("dve", lambda e: e.tensor_tensor(ex[:, l * W:(l + 1) * W], ex[:, l * W:(l + 1) * W], den[:], ALU.mult),
                         r=[Bt], w=[Bt])
                    C.op("dve", lambda e: e.tensor_tensor(LB[:, l * W:(l + 1) * W], LB[:, (l - 1) * W:l * W],
                                                          ex[:, l * W:(l + 1) * W], ALU.add), r=[Bt], w=[B_const])
                C.op("dve", lambda e: e.tensor_scalar(OML[:], LB[:], -1.0, 1.0, ALU.mult, ALU.add), r=[B_const], w=[B_const])
                stop_at(1)

                sc = sb(ph, "silu_c", [128, KC * 2], F32)
                modl = sb(ph, "modl", [128, DEPTH * OCA * 2], F32)
                bad = sb(ph, "bad", [128, DEPTH * OCA], F32)
                wa = [sb(ph, f"wa{i}", [128, D], F32) for i in range(2)]
                B_wa = [Buf("wa0"), Buf("wa1")]
                modg = sb(ph, "modg", [128, GRP * DEPTH * OCA * 2], F32)
                B_sc, B_modl, B_modg = Buf("sc"), Buf("modl"), Buf("modg")
                B_ms, B_ma = Buf("mod_src"), Buf("mod_all")
                C.dma("sp", sc[:], cvec, w=[B_sc])
                C.dma("sp", bad[:], bada, w=[B_sc])
                C.op("act", lambda e: e.activation(sc[:], sc[:], AF.Silu), r=[B_sc], w=[B_sc])
                it = 0
                for l in range(DEPTH):
                    for oc in range(OCA):
                        i = it % 2
                        it += 1
                        C.dma("sp", wa[i][:], wada[l][oc * 128:(oc + 1) * 128, :], w=[B_wa[i]])
                        half = B_ACC[it % 2][0]
                        ps = ACC[it % 2][:, 0:2]
                        mm_acc(ps, half, [(wa[i][:, kc * 128:(kc + 1) * 128], sc[:, kc * 2:(kc + 1) * 2]) for kc in range(KC)],
                               [B_wa[i], B_sc])
                        o0 = (l * OCA + oc) * 2
                        C.op("dve", lambda e: e.tensor_scalar(modl[:, o0:o0 + 2], ps, bad[:, l * OCA + oc:l * OCA + oc + 1], None, ALU.add),
                             r=[half, B_sc], w=[B_modl])
                stop_at(2)
                C.dma("sp", mod_src, modl[:], r=[B_modl], w=[B_ms])
                C.allgather(allg, mod_src, mod_all, r=[B_ms], w=[B_ma])
                C.dma("sp", apv(modg[:], 0, [[DEPTH * OCA * 2, GRP], [1, DEPTH * OCA * 2]]),
                      mod_all.rearrange("(r p) n -> p r n", p=128), r=[B_ma], w=[B_modg])
                for l in range(DEPTH):
                    for jj in range(2):
                        dst = apv(MOD[:], (l * 2 + jj) * 6 * KC, [[OCA, GRP], [1, OCA]])
                        src = apv(modg[:], l * OCA * 2 + jj, [[DEPTH * OCA * 2, GRP], [2, OCA]])
                        C.op("dve", lambda e: e.tensor_copy(dst, src), r=[B_modg], w=[B_MOD])
                        for part in (1, 4):
                            a = modcol(l, jj, part, 0)
                            a = apv(a, 0, [[1, KC]])
                            C.op("dve", lambda e: e.tensor_scalar(a, a, 1.0, None, ALU.add), w=[B_MOD])
                C.barrier()
                stop_at(3)

            for l in range(DEPTH):
                last = l == DEPTH - 1
                x_src = xT if l == 0 else xs_d
                Win, Wa_, Wb_, Wo_, Wup, Wdn = (wfull[(l, nm)] for nm in WNAMES)
                Bw = {nm: B_wfull[(l, nm)] for nm in WNAMES}

                with ExitStack() as ph:
                    ensure_weights(l - 1)
                    stop_at(30)
                    if wqueue and wqueue[0] == (l, "w_in"):
                        emit_weight()
                    stop_at(31)
                    alloc_slabs(ph, 2)
                    u_sb = sb(ph, "u_sb", [128, KC * NT], BF16)
                    B_u = Buf("u_sb")

                    def fbuf(name, dt=F32, w=NT):
                        return sb(ph, name, [128, w], dt), Buf(name)

                    ZR0 = fbuf("zr0")
                    ZS = fbuf("zs")
                    W1 = fbuf("w1")
                    W2 = fbuf("w2")
                    B128 = fbuf("b128")
                    OIN = [fbuf("oin0"), fbuf("oin1")]
                    US = [fbuf("us0", BF16), fbuf("us1", BF16)]
                    QF = fbuf("qf", BF16)
                    GS = fbuf("gs", BF16)
                    QS = fbuf("qs", BF16)
                    KF = fbuf("kf", BF16)
                    VB = fbuf("vb", BF16)
                    VS = fbuf("vs", BF16)
                    QSUB = fbuf("qsub", BF16)
                    OAB = QSUB
                    KD = fbuf("kd", BF16)
                    ATT = fbuf("att", BF16)
                    VT = fbuf("vt", BF16)
                    KDT = fbuf("kdt", BF16)
                    QEK = [fbuf("qek0", BF16), fbuf("qek1", BF16)]
                    SSTD = [fbuf("sst0", BF16), fbuf("sst1", BF16)]
                    KEYS = fbuf("keys", BF16, NCH * 320)
                    DEC = fbuf("dec", F32, NCH)
                    DECS = [fbuf("decs0", F32, NCH), fbuf("decs1", F32, NCH)]
                    BLS = fbuf("bls", F32, 1)
                    S_ = fbuf("S", F32, 128)
                    S2 = fbuf("S2", F32, 128)
                    SCTX = [fbuf("sctx0", F32, 128), fbuf("sctx1", F32, 128)]
                    XCH = fbuf("xch", F32, XW)
                    XG = fbuf("xg", F32, GRP * XW)
                    SINIT = [fbuf("sinit0", F32, 128), fbuf("sinit1", F32, 128)]
                    P3 = [ZR0, ZS, B128]
                    C.op("dve", lambda e: e.memset(ATT[0][:], 0.0), w=[ATT[1]])
                    M128 = fbuf("m128", BF16)
                    M32 = fbuf("m32", BF16)
                    C.op("pool", lambda e: e.memset(M128[0][:], 1.0), w=[M128[1]])
                    C.op("pool", lambda e: e.memset(M32[0][:], 1.0), w=[M32[1]])
                    C.op("pool", lambda e: e.memset(apv(M128[0][:], 0, [[128, NT // 128], [1, 1]]), 0.0), w=[M128[1]])
                    C.op("pool", lambda e: e.memset(apv(M32[0][:], 0, [[32, NT // 32], [1, 1]]), 0.0), w=[M32[1]])

                    stop_at(32)
                    xl = [W1, W2]
                    for kc in range(KC):
                        i = kc % 2
                        C.dma("sp", xl[i][0][:], x_src[:, kc * NT:(kc + 1) * NT], r=[B_xs], w=[xl[i][1]])
                        for jj, (a, b) in ((1, SEGS[0]), (0, SEGS[1])):
                            C.op("dve", lambda e: e.tensor_scalar(u_sb[:, kc * NT + a:kc * NT + b], xl[i][0][:, a:b],
                                                                  modcol(l, jj, 1, kc), modcol(l, jj, 0, kc), ALU.mult, ALU.add),
                                 r=[xl[i][1], B_MOD], w=[B_u])

                    stop_at(4)
                    TT = [(t0, min(512, NT - t0)) for t0 in range(0, NT, 512)]
                    acc_i = [0]

                    def proj(oc, epi):
                        def ld():
                            return load_slab(Win[oc * 128:(oc + 1) * 128, :], Bw["w_in"], KC * 128)

                        def cp(res):
                            slab, bsl = res
                            for (t0, tn) in TT:
                                a = acc_i[0] % 4
                                acc_i[0] += 1
                                ps = ACC[a][:, 0:tn]
                                bufs = B_ACC[a]
                                for kc in range(KC):
                                    C.op("pe", lambda e: e.matmul(ps, slab[:, kc * 128:(kc + 1) * 128],
                                                                  u_sb[:, kc * NT + t0:kc * NT + t0 + tn],
                                                                  start=(kc == 0), stop=(kc == KC - 1)),
                                         r=[bsl, B_u], w=bufs, sig=(kc == KC - 1))
                                epi(ps, bufs, t0, tn)
                        return (ld, cp)

                    def epi_copy(dst, eng="act", func=None):
                        def f(ps, bufs, t0, tn):
                            if func is not None:
                                C.op("act", lambda e: e.activation(dst[0][:, t0:t0 + tn], ps, func), r=bufs, w=[dst[1]])
                            elif eng == "act":
                                C.op("act", lambda e: e.copy(dst[0][:, t0:t0 + tn], ps), r=bufs, w=[dst[1]])
                            else:
                                C.op("dve", lambda e: e.tensor_copy(dst[0][:, t0:t0 + tn], ps), r=bufs, w=[dst[1]])
                        return f

                    def epi_rev(dst):
                        def f(ps, bufs, t0, tn):
                            for (a, b) in SEGS:
                                p0, p1 = max(t0, a), min(t0 + tn, b)
                                if p0 >= p1:
                                    continue
                                rv = apv(ps, p1 - t0 - 1, [[-1, p1 - p0]])
                                C.op("dve", lambda e: e.tensor_copy(dst[0][:, a + b - p1:a + b - p0], rv), r=bufs, w=[dst[1]])
                        return f

                    def rev_copy(eng, dst, src):
                        for (a, b) in SEGS:
                            rv = apv(src[0][:], b - 1, [[-1, b - a]])
                            if eng == "dve":
                                C.op("dve", lambda e: e.tensor_copy(dst[0][:, a:b], rv), r=[src[1]], w=[dst[1]])
                            else:
                                C.op("act", lambda e: e.copy(dst[0][:, a:b], rv), r=[src[1]], w=[dst[1]])

                    def c3(t, w, off=0, cw=128):
                        return apv(t[:], off, [[cw, NCH], [1, w]])

                    def gla_prep(dr, hd):
                        lbc = LB[:, (l * 2 + dr) * NH + hd:(l * 2 + dr) * NH + hd + 1]
                        omc = OML[:, (l * 2 + dr) * NH + hd:(l * 2 + dr) * NH + hd + 1]
                        if dr == 0:
                            zs, qs, vs = ZR0, QF, VB
                        else:
                            rev_copy("act", QS, QF)
                            rev_copy("dve", VS, VB)
                            zs, qs, vs = ZS, QS, VS
                        C.op("act", lambda e: e.activation(W1[0][:], zs[0][:], AF.Exp, scale=-1.0), r=[zs[1]], w=[W1[1]])
                        C.op("act", lambda e: e.activation(W2[0][:], W1[0][:], AF.Ln, bias=1.0), r=[W1[1]], w=[W2[1]])
                        C.op("act", lambda e: e.activation(W2[0][:], W2[0][:], AF.Exp, scale=-1.0), w=[W2[1]])
                        C.op("dve", lambda e: e.tensor_scalar(W1[0][:], W2[0][:], omc, lbc, ALU.mult, ALU.add),
                             r=[W2[1], B_const], w=[W1[1]])
                        C.op("act", lambda e: e.activation(W2[0][:], W1[0][:], AF.Ln), r=[W1[1]], w=[W2[1]])
                        C.op("dve", lambda e: e.tensor_scalar(KF[0][:], W1[0][:], -1.0, 1.0, ALU.mult, ALU.add), r=[W1[1]], w=[KF[1]])
                        C.op("dve", lambda e: e.tensor_tensor_scan(B128[0][:], M128[0][:], W2[0][:], 0.0, ALU.mult, ALU.add),
                             r=[W2[1], M128[1]], w=[B128[1]])
                        C.op("dve", lambda e: e.tensor_tensor_scan(W1[0][:], M32[0][:], W2[0][:], 0.0, ALU.mult, ALU.add),
                             r=[W2[1], M32[1]], w=[W1[1]])
                        C.op("act", lambda e: e.activation(W1[0][:], W1[0][:], AF.Exp), r=[W1[1]], w=[W1[1]])
                        C.op("dve", lambda e: e.tensor_tensor(QSUB[0][:], qs[0][:], W1[0][:], ALU.mult), r=[qs[1], W1[1]], w=[QSUB[1]])
                        C.op("act", lambda e: e.activation(W2[0][:], B128[0][:], AF.Exp), r=[B128[1]], w=[W2[1]])
                        C.op("dve", lambda e: e.tensor_tensor(QEK[dr][0][:], qs[0][:], W2[0][:], ALU.mult), r=[qs[1], W2[1]], w=[QEK[dr][1]])
                        C.op("dve", lambda e: e.tensor_copy(DECS[dr][0][:], apv(W2[0][:], 127, [[128, NCH]])),
                             r=[W2[1]], w=[DECS[dr][1]])
                        C.op("dve", lambda e: e.tensor_reduce(BLS[0][:], apv(B128[0][:], CTX + 127, [[128, NLC]]), AX.X, ALU.add),
                             r=[B128[1]], w=[BLS[1]])
                        C.op("act", lambda e: e.activation(XCH[0][:, 256 + dr:257 + dr], BLS[0][:], AF.Exp), r=[BLS[1]], w=[XCH[1]])
                        bl_b = apv(B128[0][:], 127, [[128, NCH], [0, 128]])
                        C.op("dve", lambda e: e.tensor_tensor(c3(W1[0], 128), bl_b, c3(B128[0], 128), ALU.subtract),
                             r=[B128[1]], w=[W1[1]])
                        C.op("act", lambda e: e.activation(W1[0][:], W1[0][:], AF.Exp), r=[W1[1]], w=[W1[1]])
                        C.op("dve", lambda e: e.tensor_tensor(KD[0][:], KF[0][:], W1[0][:], ALU.mult), r=[KF[1], W1[1]], w=[KD[1]])
                        for I in range(4):
                            wI = 32 * (I + 1)
                            offI = (0, 32, 96, 192)[I]
                            tmp = W1 if I % 2 == 0 else W2
                            dstE = apv(tmp[0][:], 0, [[wI, NCH], [1, wI]])
                            if I == 0:
                                C.op("dve", lambda e: e.tensor_scalar(dstE, c3(B128[0], wI), -1.0, None, ALU.mult),
                                     r=[B128[1]], w=[tmp[1]])
                            else:
                                rI = apv(B128[0][:], 32 * I - 1, [[128, NCH], [0, wI]])
                                C.op("dve", lambda e: e.tensor_tensor(dstE, rI, c3(B128[0], wI), ALU.subtract),
                                     r=[B128[1]], w=[tmp[1]])
                            C.op("act", lambda e: e.activation(tmp[0][:, 0:NCH * wI], tmp[0][:, 0:NCH * wI], AF.Exp), w=[tmp[1]])
                            C.op("dve", lambda e: e.tensor_tensor(c3(KEYS[0], wI, offI, 320), dstE, c3(KF[0], wI), ALU.mult),
                                 r=[tmp[1], KF[1]], w=[KEYS[1]])
                        for c0 in range(0, NCH, 4):
                            ncg = min(4, NCH - c0)
                            for which, (src_, dst_) in enumerate(((vs, VT), (KD, KDT))):
                                slot = which
                                for cc in range(ncg):
                                    c = c0 + cc
                                    C.op("pe", lambda e: e.transpose(PSTR[:, slot * 512 + cc * 128:slot * 512 + (cc + 1) * 128],
                                                                     src_[0][:, c * 128:(c + 1) * 128], ident[:]),
                                         r=[src_[1], B_const], w=[B_PSTR[slot]], sig=(cc == ncg - 1))
                                o_ap = dst_[0][:, c0 * 128:(c0 + ncg) * 128]
                                i_ap = PSTR[:, slot * 512:slot * 512 + ncg * 128]
                                if which == 0:
                                    C.op("act", lambda e: e.copy(o_ap, i_ap), r=[B_PSTR[slot]], w=[dst_[1]])
                                else:
                                    C.op("dve", lambda e: e.tensor_copy(o_ap, i_ap), r=[B_PSTR[slot]], w=[dst_[1]])
                            for cc in range(ncg):
                                c = c0 + cc
                                C.op("pe", lambda e: e.matmul(PSA[:, cc * 128:(cc + 1) * 128], KDT[0][:, c * 128:(c + 1) * 128],
                                                              VT[0][:, c * 128:(c + 1) * 128], start=True, stop=True),
                                     r=[KDT[1], VT[1]], w=[B_PSA], sig=(cc == ncg - 1))
                            C.op("act", lambda e: e.copy(US[dr][0][:, c0 * 128:(c0 + ncg) * 128], PSA[:, 0:ncg * 128]),
                                 r=[B_PSA], w=[US[dr][1]])
                            for cc in range(ncg):
                                c = c0 + cc
                                for I in range(4):
                                    wI = 32 * (I + 1)
                                    offI = (0, 32, 96, 192)[I]
                                    C.op("pe", lambda e: e.matmul(PSB[0:wI, cc * 128 + 32 * I:cc * 128 + 32 * I + 32],
                                                                  KEYS[0][:, c * 320 + offI:c * 320 + offI + wI],
                                                                  QSUB[0][:, c * 128 + 32 * I:c * 128 + 32 * I + 32],
                                                                  start=True, stop=True),
                                         r=[KEYS[1], QSUB[1]], w=[B_PSB], sig=(cc == ncg - 1 and I == 3))
                            for I in range(4):
                                wI = 32 * (I + 1)
                                o_ap = apv(ATT[0][0:wI, :], c0 * 128 + 32 * I, [[128, ncg], [1, 32]])
                                i_ap = apv(PSB[0:wI, :], 32 * I, [[128, ncg], [1, 32]])
                                m_ap = apv(cmask[0:wI, :], 32 * I, [[0, ncg], [1, 32]])
                                C.op("dve", lambda e: e.tensor_tensor(o_ap, i_ap, m_ap, ALU.mult), r=[B_PSB, B_const], w=[ATT[1]])
                            for cc in range(ncg):
                                c = c0 + cc
                                C.op("pe", lambda e: e.matmul(PSA[:, cc * 128:(cc + 1) * 128], VT[0][:, c * 128:(c + 1) * 128],
                                                              ATT[0][:, c * 128:(c + 1) * 128], start=True, stop=True),
                                     r=[VT[1], ATT[1]], w=[B_PSA], sig=(cc == ncg - 1))
                            C.op("act", lambda e: e.copy(OIN[dr][0][:, c0 * 128:(c0 + ncg) * 128], PSA[:, 0:ncg * 128]),
                                 r=[B_PSA], w=[OIN[dr][1]])

                    def recur(dr, c_list, S, save_start=None):
                        first = True
                        for c in c_list:
                            if save_start is not None:
                                if first:
                                    C.op("dve", lambda e: e.memset(save_start[0][:, c * 128:(c + 1) * 128], 0.0), w=[save_start[1]])
                                else:
                                    C.op("act", lambda e: e.copy(save_start[0][:, c * 128:(c + 1) * 128], S[0][:]),
                                         r=[S[1]], w=[save_start[1]])
                            u_ap = US[dr][0][:, c * 128:(c + 1) * 128]
                            if first:
                                C.op("dve", lambda e: e.tensor_copy(S[0][:], u_ap), r=[US[dr][1]], w=[S[1]])
                            else:
                                C.op("dve", lambda e: e.scalar_tensor_tensor(S[0][:], S[0][:], DECS[dr][0][:, c:c + 1], u_ap,
                                                                             ALU.mult, ALU.add),
                                     r=[US[dr][1], DECS[dr][1]], w=[S[1]])
                            first = False

                    GSB = [GS, fbuf("gs2", BF16)]

                    def head_jobs(hd):
                        return [
                            proj(0 * NH + hd, epi_copy(ZR0, "act")),
                            proj(1 * NH + hd, epi_rev(ZS)),
                            proj(2 * NH + hd, epi_copy(VB, "dve")),
                            proj(3 * NH + hd, epi_copy(QF, func=AF.Silu)),
                            proj(4 * NH + hd, epi_copy(GSB[hd % 2], func=AF.Silu)),
                        ]

                    run_jobs(head_jobs(0))
                    for hd in range(NH):
                        GS = GSB[hd % 2]
                        stop_at(5)
                        for dr in range(2):
                            gla_prep(dr, hd)
                            stop_at(6)
                            recur(dr, list(range(NCC)), SCTX[dr], SSTD[dr])
                            recur(dr, list(range(NCC, NCH)), S_, None)
                            C.op("dve", lambda e: e.tensor_copy(XCH[0][:, dr * 128:(dr + 1) * 128], S_[0][:]), r=[S_[1]], w=[XCH[1]])
                        stop_at(7)
                        xi = l * NH + hd
                        Bsrc, Bdst = Buf("xsrc"), Buf("xdst")
                        C.dma("sp", xch_src[xi], XCH[0][:], r=[XCH[1]], w=[Bsrc])
                        C.allgather(allg, xch_src[xi], xch_dst[xi], r=[Bsrc], w=[Bdst])
                        emit_weight()
                        C.dma("sp", apv(XG[0][:], 0, [[XW, GRP], [1, XW]]), xch_dst[xi].rearrange("(r p) n -> p r n", p=128),
                              r=[Bdst], w=[XG[1]])
                        if hd + 1 < NH:
                            run_jobs(head_jobs(hd + 1))
                        for dr in range(2):
                            Sg = SINIT[dr]
                            C.op("dve", lambda e: e.tensor_copy(Sg[0][:], SCTX[dr][0][:]), r=[SCTX[dr][1]], w=[Sg[1]])
                            order = range(GRP) if dr == 0 else range(GRP - 1, -1, -1)
                            for i in order:
                                Sl = XG[0][:, i * XW + dr * 128:i * XW + (dr + 1) * 128]
                                Dl = XG[0][:, i * XW + 256 + dr:i * XW + 257 + dr]
                                mk = SEL[:, dr * 4 + i:dr * 4 + i + 1]
                                C.op("dve", lambda e: e.scalar_tensor_tensor(S2[0][:], Sg[0][:], Dl, Sl, ALU.mult, ALU.add),
                                     r=[Sg[1], XG[1]], w=[S2[1]])
                                C.op("dve", lambda e: e.tensor_tensor(S2[0][:], S2[0][:], Sg[0][:], ALU.subtract), r=[Sg[1]], w=[S2[1]])
                                C.op("dve", lambda e: e.scalar_tensor_tensor(Sg[0][:], S2[0][:], mk, Sg[0][:], ALU.mult, ALU.add),
                                     r=[S2[1], B_const], w=[Sg[1]])
                            for c in range(NCC, NCH):
                                C.op("act", lambda e: e.copy(SSTD[dr][0][:, c * 128:(c + 1) * 128], Sg[0][:]),
                                     r=[Sg[1]], w=[SSTD[dr][1]])
                                if c < NCH - 1:
                                    C.op("dve", lambda e: e.scalar_tensor_tensor(Sg[0][:], Sg[0][:], DECS[dr][0][:, c:c + 1],
                                                                                 US[dr][0][:, c * 128:(c + 1) * 128], ALU.mult, ALU.add),
                                         r=[US[dr][1], DECS[dr][1]], w=[Sg[1]])
                            for c0 in range(0, NCH, 4):
                                ncg = min(4, NCH - c0)
                                for cc in range(ncg):
                                    c = c0 + cc
                                    C.op("pe", lambda e: e.matmul(PSA[:, cc * 128:(cc + 1) * 128], SSTD[dr][0][:, c * 128:(c + 1) * 128],
                                                                  QEK[dr][0][:, c * 128:(c + 1) * 128], start=True, stop=True),
                                         r=[SSTD[dr][1], QEK[dr][1]], w=[B_PSA], sig=(cc == ncg - 1))
                                o_ap = OIN[dr][0][:, c0 * 128:(c0 + ncg) * 128]
                                C.op("dve", lambda e: e.tensor_tensor(o_ap, o_ap, PSA[:, 0:ncg * 128], ALU.add),
                                     r=[B_PSA], w=[OIN[dr][1]])
                        OSUM = OIN[0]
                        for (a, b) in SEGS:
                            rv = apv(OIN[1][0][:], b - 1, [[-1, b - a]])
                            C.op("dve", lambda e: e.tensor_tensor(OSUM[0][:, a:b], OSUM[0][:, a:b], rv, ALU.add), r=[OIN[1][1]], w=[OSUM[1]])
                        stop_at(8)
                        C.op("act", lambda e: e.activation(W1[0][:], OSUM[0][:], AF.Square), r=[OSUM[1]], w=[W1[1]])
                        for (t0, tn) in TT:
                            C.op("pe", lambda e: e.matmul(PSS[:, 0:tn], ones_f[:], W1[0][:, t0:t0 + tn], start=True, stop=True),
                                 r=[W1[1], B_const], w=[B_PSS])
                            C.op("dve", lambda e: e.tensor_scalar(W2[0][:, t0:t0 + tn], PSS[:, 0:tn], 1.0 / 128.0, RMS_EPS, ALU.mult, ALU.add),
                                 r=[B_PSS], w=[W2[1]])
                        C.op("act", lambda e: e.activation(W2[0][:], W2[0][:], AF.Ln), w=[W2[1]])
                        C.op("act", lambda e: e.activation(W2[0][:], W2[0][:], AF.Exp, scale=-0.5), w=[W2[1]])
                        C.op("dve", lambda e: e.tensor_tensor(W2[0][:], W2[0][:], OSUM[0][:], ALU.mult), r=[OSUM[1]], w=[W2[1]])
                        C.op("dve", lambda e: e.scalar_tensor_tensor(OAB[0][:], W2[0][:], GN[:, l:l + 1], GS[0][:], ALU.mult, ALU.mult),
                             r=[W2[1], GS[1], B_const], w=[OAB[1]])
                        C.dma("sp", oa_d[:, hd * NT:(hd + 1) * NT], OAB[0][:], r=[OAB[1]], w=[B_oa])

                    stop_at(9)
                    nrow = NL // cfg.GRID_W
                    for cb in range(NH):
                        jobs = [proj(5 * NH + j * NH + cb, epi_copy(P3[j], "act" if j != 1 else "dve")) for j in range(3)]
                        run_jobs(jobs)
                        Bg, Cg, Hh = P3
                        Z, Y = W1, W2
                        C.op("dve", lambda e: e.tensor_tensor(Z[0][:], Cg[0][:], Hh[0][:], ALU.mult), r=[Cg[1], Hh[1]], w=[Z[1]])
                        cw = lambda j: CW[:, (l * NH + cb) * 3 + j:(l * NH + cb) * 3 + j + 1]
                        C.op("dve", lambda e: e.tensor_scalar(Y[0][:], Z[0][:], cw(1), None, ALU.mult), r=[Z[1], B_const], w=[Y[1]])
                        for (a, n_r, rl) in ((0, 1, CTX), (CTX, nrow, cfg.GRID_W)):
                            ya = apv(Y[0][:], a + 1, [[rl, n_r], [1, rl - 1]])
                            za = apv(Z[0][:], a, [[rl, n_r], [1, rl - 1]])
                            C.op("dve", lambda e: e.scalar_tensor_tensor(ya, za, cw(0), ya, ALU.mult, ALU.add), r=[Z[1]], w=[Y[1]])
                            yb = apv(Y[0][:], a, [[rl, n_r], [1, rl - 1]])
                            zb_ = apv(Z[0][:], a + 1, [[rl, n_r], [1, rl - 1]])
                            C.op("dve", lambda e: e.scalar_tensor_tensor(yb, zb_, cw(2), yb, ALU.mult, ALU.add), r=[Z[1]], w=[Y[1]])
                        C.op("dve", lambda e: e.tensor_tensor(OAB[0][:], Bg[0][:], Y[0][:], ALU.mult), r=[Bg[1], Y[1]], w=[OAB[1]])
                        C.dma("sp", zb_d[:, cb * NT:(cb + 1) * NT], OAB[0][:], r=[OAB[1]], w=[B_zb])
                    C.barrier()

                with ExitStack() as ph:
                    def tb(name, w, dt=F32):
                        return sb(ph, name, [128, w], dt), Buf(name)
                    alloc_slabs(ph, 3)
                    HC2 = DFFC // 2
                    XT = tb("xt", KC * T)
                    UT = tb("ut", KC * T, BF16)
                    HT = tb("ht", HC2 * T, BF16)
                    YT = (HT[0][:, 0:KC * T], HT[1])
                    OAT = (HT[0][:, KC * T:(KC + NH) * T], HT[1])
                    ZBT = (HT[0][:, (KC + NH) * T:(KC + 2 * NH) * T], HT[1])
                    R0, R1, R2, R3 = tb("r0", T), tb("r1", T), tb("r2", T), tb("r3", T)
                    SA, SB_ = R0, R1
                    TM2 = [R0, R1]
                    TMP = [R0, R1]
                    MEAN, RSTD = R2, R3
                    cnt = [0]

                    def acc_slot():
                        a = cnt[0] % 4
                        cnt[0] += 1
                        return ACC[a], B_ACC[a][0]

                    def mm(ps, bps, pairs, rbufs, start=True, stop=True):
                        n = len(pairs)
                        for i, (l_, r_) in enumerate(pairs):
                            C.op("pe", lambda e: e.matmul(ps, l_, r_, start=(start and i == 0), stop=(stop and i == n - 1)),
                                 r=rbufs, w=[bps], sig=(i == n - 1))

                    def layer_norm(which, Tt, dst_fn):
                        for kc in range(KC):
                            i = kc % 2
                            v_ap = XT[0][:, kc * T:kc * T + Tt]
                            C.op("act", lambda e: e.activation(TMP[i][0][:, 0:Tt], v_ap, AF.Square), r=[XT[1]], w=[TMP[i][1]])
                            C.op("pe", lambda e: e.matmul(PSS[:, 0:Tt], ones_f[:], v_ap, start=(kc == 0), stop=(kc == KC - 1)),
                                 r=[XT[1], B_const], w=[B_PSS], sig=False)
                            C.op("pe", lambda e: e.matmul(PSA[:, 0:Tt], ones_f[:], TMP[i][0][:, 0:Tt], start=(kc == 0), stop=(kc == KC - 1)),
                                 r=[TMP[i][1]], w=[B_PSA], sig=True)
                        mean, rstd = MEAN[0][:, 0:Tt], RSTD[0][:, 0:Tt]
                        C.op("dve", lambda e: e.tensor_scalar(mean, PSS[:, 0:Tt], 1.0 / D, None, ALU.mult), r=[B_PSS], w=[MEAN[1]])
                        C.op("dve", lambda e: e.tensor_tensor(rstd, mean, mean, ALU.mult), r=[MEAN[1]], w=[RSTD[1]])
                        C.op("dve", lambda e: e.scalar_tensor_tensor(rstd, PSA[:, 0:Tt], 1.0 / D, rstd, ALU.mult, ALU.subtract),
                             r=[B_PSA], w=[RSTD[1]])
                        C.op("dve", lambda e: e.tensor_scalar(rstd, rstd, LN_EPS, None, ALU.add), w=[RSTD[1]])
                        C.op("act", lambda e: e.activation(rstd, rstd, AF.Sqrt), w=[RSTD[1]])
                        C.op("dve", lambda e: e.reciprocal(rstd, rstd), w=[RSTD[1]])
                        for kc in range(KC):
                            v_ap = XT[0][:, kc * T:kc * T + Tt]
                            gi = (l * 2 + which) * KC + kc
                            C.op("dve", lambda e: e.tensor_tensor(v_ap, v_ap, mean, ALU.subtract), r=[MEAN[1]], w=[XT[1]])
                            C.op("dve", lambda e: e.tensor_tensor(v_ap, v_ap, rstd, ALU.mult), r=[RSTD[1]], w=[XT[1]])
                            C.op("act", lambda e: e.activation(v_ap, v_ap, AF.Identity, bias=LNB[:, gi:gi + 1], scale=LNG[:, gi:gi + 1]),
                                 r=[B_const], w=[XT[1]])
                            dst_fn(kc)

                    tiles = [(t0, min(T, CTX - t0)) for t0 in range(0, CTX, T)] + [(t0, min(T, NT - t0)) for t0 in range(CTX, NT, T)]
                    for (t0, Tt) in tiles:
                        is_ctx = t0 < CTX
                        jj = 1 if is_ctx else 0
                        if last and is_ctx:
                            continue
                        C.dma("sp", apv(XT[0][:], 0, [[T, KC], [1, Tt]]), apv(x_src, t0, [[NT, KC], [1, Tt]]), r=[B_xs], w=[XT[1]])
                        C.dma("sp", apv(OAT[0], 0, [[T, NH], [1, Tt]]), apv(oa_d, t0, [[NT, NH], [1, Tt]]), r=[B_oa], w=[OAT[1]])
                        C.dma("sp", apv(ZBT[0], 0, [[T, NH], [1, Tt]]), apv(zb_d, t0, [[NT, NH], [1, Tt]]), r=[B_zb], w=[ZBT[1]])
                        for kc in range(KC):
                            C.op("dve", lambda e: e.tensor_scalar(UT[0][:, kc * T:kc * T + Tt], XT[0][:, kc * T:kc * T + Tt],
                                                                  modcol(l, jj, 1, kc), modcol(l, jj, 0, kc), ALU.mult, ALU.add),
                                 r=[XT[1], B_MOD], w=[UT[1]])
                        sa, sb_ = SA[0][:, 0:Tt], SB_[0][:, 0:Tt]
                        jobs = []
                        for fc in range(KC):
                            g0 = (8 * NH + fc) * 128
                            g1 = (8 * NH + KC + fc) * 128
                            specs = ((Win[g0:g0 + 128, :], Bw["w_in"], KC, UT, 0), (Win[g1:g1 + 128, :], Bw["w_in"], KC, UT, 1),
                                     (Wa_[fc * 128:(fc + 1) * 128, :], Bw["w_out_a"], NH, OAT, 2),
                                     (Wb_[fc * 128:(fc + 1) * 128, :], Bw["w_out_b"], NH, ZBT, 3))
                            for (wsrc, wb_, nk, rhs, kind) in specs:
                                def ld(wsrc=wsrc, wb_=wb_, nk=nk):
                                    return load_slab(wsrc, wb_, nk * 128)

                                def cp(res, nk=nk, rhs=rhs, kind=kind, fc=fc):
                                    slab, bsl = res
                                    acc, bps = acc_slot()
                                    ps = acc[:, 0:Tt]
                                    rap = rhs[0] if isinstance(rhs[0], bass.AP) else rhs[0][:]
                                    mm(ps, bps, [(slab[:, k * 128:(k + 1) * 128], apv(rap, k * T, [[1, Tt]])) for k in range(nk)], [bsl, rhs[1]])
                                    if kind == 0:
                                        C.op("act", lambda e: e.activation(sa, ps, AF.Sigmoid), r=[bps], w=[SA[1]])
                                    elif kind == 1:
                                        C.op("act", lambda e: e.activation(sb_, ps, AF.Sigmoid), r=[bps], w=[SB_[1]])
                                    elif kind == 2:
                                        C.op("dve", lambda e: e.tensor_tensor(sa, sa, ps, ALU.mult), r=[bps], w=[SA[1]])
                                    else:
                                        C.op("dve", lambda e: e.tensor_tensor(sb_, sb_, ps, ALU.mult), r=[bps], w=[SB_[1]])
                                        C.op("dve", lambda e: e.tensor_tensor(apv(YT[0], fc * T, [[1, Tt]]), sa, sb_, ALU.add),
                                             r=[SA[1], SB_[1]], w=[YT[1]])
                                jobs.append((ld, cp))
                        run_jobs(jobs, depth=2)
                        jobs = []
                        for fc in range(KC):
                            def ld(fc=fc):
                                return load_slab(Wo_[fc * 128:(fc + 1) * 128, :], Bw["w_o"], KC * 128)

                            def cp(res, fc=fc):
                                slab, bsl = res
                                i = fc % 2
                                acc, bps = acc_slot()
                                ps = acc[:, 0:Tt]
                                mm(ps, bps, [(slab[:, k * 128:(k + 1) * 128], apv(YT[0], k * T, [[1, Tt]])) for k in range(KC)], [bsl, YT[1]])
                                tm = TM2[i][0][:, 0:Tt]
                                C.op("act", lambda e: e.activation(tm, ps, AF.Copy, scale=modcol(l, jj, 2, fc)),
                                     r=[bps, B_MOD], w=[TM2[i][1]])
                                x_ap = XT[0][:, fc * T:fc * T + Tt]
                                C.op("dve", lambda e: e.scalar_tensor_tensor(x_ap, x_ap, ALPHA, tm, ALU.mult, ALU.add),
                                     r=[TM2[i][1]], w=[XT[1]])
                            jobs.append((ld, cp))
                        run_jobs(jobs, depth=2)

                        def mk_u2(kc):
                            C.op("dve", lambda e: e.tensor_scalar(UT[0][:, kc * T:kc * T + Tt], XT[0][:, kc * T:kc * T + Tt],
                                                                  modcol(l, jj, 4, kc), modcol(l, jj, 3, kc), ALU.mult, ALU.add),
                                 r=[XT[1], B_MOD], w=[UT[1]])
                        layer_norm(0, Tt, mk_u2)
                        jobs = []
                        NQH = HC2 // KC
                        for hf in range(2):
                            for hh in range(HC2):
                                hc = hf * HC2 + hh

                                def ld(hc=hc):
                                    return load_slab(Wup[hc * 128:(hc + 1) * 128, :], Bw["w_up"], KC * 128)

                                def cp(res, hh=hh):
                                    slab, bsl = res
                                    i = hh % 2
                                    acc, bps = acc_slot()
                                    ps = acc[:, 0:Tt]
                                    mm(ps, bps, [(slab[:, k * 128:(k + 1) * 128], UT[0][:, k * T:k * T + Tt]) for k in range(KC)], [bsl, UT[1]])
                                    tm = TM2[i][0][:, 0:Tt]
                                    C.op("act", lambda e: e.activation(tm, ps, AF.Relu), r=[bps], w=[TM2[i][1]])
                                    C.op("dve", lambda e: e.tensor_tensor(HT[0][:, hh * T:hh * T + Tt], tm, tm, ALU.mult),
                                         r=[TM2[i][1]], w=[HT[1]])
                                jobs.append((ld, cp))
                            for fc in range(KC):
                                state = {}
                                for q in range(NQH):
                                    qg = hf * NQH + q

                                    def ld(fc=fc, qg=qg):
                                        return load_slab(Wdn[fc * 128:(fc + 1) * 128, qg * KC * 128:(qg + 1) * KC * 128], Bw["w_down"], KC * 128)

                                    def cp(res, fc=fc, q=q, hf=hf, state=state):
                                        slab, bsl = res
                                        if q == 0:
                                            state["acc"] = acc_slot()
                                        acc, bps = state["acc"]
                                        ps = acc[:, 0:Tt]
                                        mm(ps, bps, [(slab[:, k * 128:(k + 1) * 128], HT[0][:, (q * KC + k) * T:(q * KC + k) * T + Tt]) for k in range(KC)],
                                           [bsl, HT[1]], start=(q == 0), stop=(q == NQH - 1))
                                        if q == NQH - 1:
                                            i = fc % 2
                                            tm = TM2[i][0][:, 0:Tt]
                                            C.op("act", lambda e: e.activation(tm, ps, AF.Copy, scale=modcol(l, jj, 5, fc)),
                                                 r=[bps, B_MOD], w=[TM2[i][1]])
                                            x_ap = XT[0][:, fc * T:fc * T + Tt]
                                            if hf == 0:
                                                C.op("dve", lambda e: e.scalar_tensor_tensor(x_ap, x_ap, ALPHA, tm, ALU.mult, ALU.add),
                                                     r=[TM2[i][1]], w=[XT[1]])
                                            else:
                                                C.op("dve", lambda e: e.tensor_tensor(x_ap, x_ap, tm, ALU.add), r=[TM2[i][1]], w=[XT[1]])
                                    jobs.append((ld, cp))
                        run_jobs(jobs, depth=2)
                        layer_norm(1, Tt, lambda kc: None)
                        if not last:
                            C.dma("sp", apv(xs_d, t0, [[NT, KC], [1, Tt]]), apv(XT[0][:], 0, [[T, KC], [1, Tt]]), r=[XT[1]], w=[B_xs])
                        else:
                            C.dma("sp", apv(outT, t0 - CTX, [[NL, KC], [1, Tt]]), apv(XT[0][:], 0, [[T, KC], [1, Tt]]), r=[XT[1]], w=[Buf("out")])
                    C.barrier()
        except _Stop:
            pass
        STOPPED[0] = False
        C.barrier()
    return nc


def fm(a, ):
    F, n = a.shape
    return np.ascontiguousarray(a.reshape(F // 128, 128, n).transpose(1, 0, 2).reshape(128, (F // 128) * n))


def wblock(W):
    K, N = W.shape
    return np.ascontiguousarray(W.reshape(K // 128, 128, N // 128, 128).transpose(2, 1, 0, 3).reshape(N, K))


def prepare_inputs(cfg, x, c, ctx, c_ctx, w_ada, b_ada, w_in, conv_w, gnorm_g, w_out_a, w_out_b, w_o,
                   w_up, w_down, ln_g, ln_b, lower_bounds):
    f32 = np.float32
    D, KC, NH, DEPTH, NL, CTX = cfg.D, cfg.KC, cfg.NH, cfg.DEPTH, cfg.NL, cfg.CTX
    OCA = 6 * KC // GRP
    ws = {"w_in": w_in, "w_out_a": w_out_a, "w_out_b": w_out_b, "w_o": w_o, "w_up": w_up, "w_down": w_down}
    shared = {}
    wblk = {}
    for l in range(DEPTH):
        for nm in WNAMES:
            wblk[(l, nm)] = wblock(np.asarray(ws[nm][l], f32))
    wada_blk = [wblock(np.asarray(w_ada[l], f32)) for l in range(DEPTH)]
    cw = np.asarray(conv_w, f32)
    shared["conv_w"] = np.ascontiguousarray(
        cw.reshape(DEPTH, 3, NH, 128).transpose(3, 0, 2, 1).reshape(128, DEPTH * NH * 3))
    shared["gnorm_g"] = np.ascontiguousarray(np.asarray(gnorm_g, f32).T)
    shared["ln_g"] = np.ascontiguousarray(np.asarray(ln_g, f32).reshape(DEPTH, 2, KC, 128).transpose(3, 0, 1, 2).reshape(128, -1))
    shared["ln_b"] = np.ascontiguousarray(np.asarray(ln_b, f32).reshape(DEPTH, 2, KC, 128).transpose(3, 0, 1, 2).reshape(128, -1))
    shared["lbraw"] = np.ascontiguousarray(
        np.asarray(lower_bounds, f32).reshape(DEPTH, 2, NH, 128).transpose(3, 0, 1, 2).reshape(128, -1))
    in_maps = []
    for r in range(NCORES):
        b, j = r // GRP, r % GRP
        m = dict(shared)
        xt = np.concatenate([np.asarray(ctx[b], f32).T, np.asarray(x[b, j * NL:(j + 1) * NL], f32).T], axis=1)
        m["xT"] = fm(xt)
        m["cvec"] = fm(np.stack([np.asarray(c[b], f32), np.asarray(c_ctx, f32)], axis=1))
        s = np.zeros((128, 8), f32)
        for i in range(GRP):
            s[:, i] = 1.0 if i < j else 0.0
            s[:, 4 + i] = 1.0 if i > j else 0.0
        m["sel"] = s
        for l in range(DEPTH):
            for nm in WNAMES:
                blk = wblk[(l, nm)]
                m[f"{nm}_{l}"] = blk
            m[f"w_ada_{l}"] = wada_blk[l][j * OCA * 128:(j + 1) * OCA * 128]
        ba = np.asarray(b_ada, f32).reshape(DEPTH, GRP, OCA, 128)[:, j]
        m["b_ada"] = np.ascontiguousarray(ba.transpose(2, 0, 1).reshape(128, DEPTH * OCA))
        in_maps.append(m)
    return in_maps


def assemble_output(cfg, results):
    D, KC, NL = cfg.D, cfg.KC, cfg.NL
    out = np.zeros((cfg.BATCH, cfg.SEQ, D), np.float32)
    for r in range(NCORES):
        b, j = r // GRP, r % GRP
        o = np.asarray(results[r]["outT"]).reshape(128, KC, NL).transpose(2, 1, 0).reshape(NL, D)
        out[b, j * NL:(j + 1) * NL] = o
    return out


def run(cfg, **inputs):
    nc = build_program(cfg)
    in_maps = prepare_inputs(cfg, **inputs)
    res = run_bass_kernel_spmd(nc, in_maps, core_ids=list(range(NCORES)))
    return assemble_output(cfg, res.results)


def kernel(**inputs):
    return run(Cfg(), **inputs)
```
